# Optimizing a Trainium2 kernel written in Bass

```python
import math
import jax, jax.numpy as jnp
from jax import lax
import numpy as np

D_MODEL = 1024
BATCH = 16
SEQ = 4096
DEPTH = 1

SSM_WIDTH = D_MODEL
SSM_GROUP = 16
SSM_GROUPS = SSM_WIDTH // SSM_GROUP
SSM_STATE = 64
SSM_CHUNK = 128
DT_MIN = 1e-3
DT_MAX = 1e-1
HEAD_DIM = 64
HEADS_PER_GROUP = D_MODEL // 128
DILATED_GROUPS = ((128, 1), (512, 4), (2048, 16))
N_DIL = len(DILATED_GROUPS)
ATTN_WIDTH = HEADS_PER_GROUP * HEAD_DIM
QKV_WIDTH = N_DIL * ATTN_WIDTH
ROPE_DIM = HEAD_DIM // 4
ROPE_THETA = 500000.0
Q_BLOCK = 128
NEG_INF = -1e30
EPS = 1e-6
SPLITS = (SSM_WIDTH, SSM_WIDTH, QKV_WIDTH, QKV_WIDTH, QKV_WIDTH, ATTN_WIDTH, D_MODEL, D_MODEL)
IN_WIDTH = sum(SPLITS)

kernel_name = "hybrid_s5_dilated_attn_gated_block"


def rms_norm(x, w):
    xf = x.astype(jnp.float32)
    y = xf * lax.rsqrt(jnp.mean(xf * xf, axis=-1, keepdims=True) + EPS)
    return y * w.astype(jnp.float32)


def apply_partial_rope(x, positions):
    half = ROPE_DIM // 2
    inv_freq = ROPE_THETA ** (-jnp.arange(0, ROPE_DIM, 2, dtype=jnp.float32) / ROPE_DIM)
    ang = positions.astype(jnp.float32)[..., None] * inv_freq
    cos = jnp.cos(ang)[:, :, None, None, :]
    sin = jnp.sin(ang)[:, :, None, None, :]
    x1 = x[..., :half]
    x2 = x[..., half:ROPE_DIM]
    return jnp.concatenate([x1 * cos - x2 * sin, x2 * cos + x1 * sin, x[..., ROPE_DIM:]], axis=-1)


def _complex_diag_combine(e1, e2):
    a1r, a1i, b1r, b1i = e1
    a2r, a2i, b2r, b2i = e2
    ar = a2r * a1r - a2i * a1i
    ai = a2r * a1i + a2i * a1r
    br = a2r * b1r - a2i * b1i + b2r
    bi = a2r * b1i + a2i * b1r + b2i
    return (ar, ai, br, bi)


def s5_branch(u, lam_re, lam_im, log_dt, b_re, b_im, c_re, c_im, d_skip, w_glu):
    f32 = jnp.float32
    uf = u.astype(f32)
    lr, li = lam_re.astype(f32), lam_im.astype(f32)
    dt = jnp.exp(log_dt.astype(f32))[:, None]
    mag = jnp.exp(lr * dt)
    ab_re = mag * jnp.cos(li * dt)
    ab_im = mag * jnp.sin(li * dt)
    den = lr * lr + li * li
    nr = ab_re - 1.0
    f_re = (nr * lr + ab_im * li) / den
    f_im = (ab_im * lr - nr * li) / den
    br, bi = b_re.astype(f32), b_im.astype(f32)
    bb_re = f_re[..., None] * br - f_im[..., None] * bi
    bb_im = f_re[..., None] * bi + f_im[..., None] * br
    cr, ci = c_re.astype(f32), c_im.astype(f32)

    bsz, length, _ = u.shape
    n_chunks = length // SSM_CHUNK
    uc = uf.reshape(bsz, n_chunks, SSM_CHUNK, SSM_GROUPS, SSM_GROUP).transpose(1, 0, 2, 3, 4)

    def chunk_step(carry, u_c):
        h0_re, h0_im = carry
        bu_re = jnp.einsum('blgp,gnp->blgn', u_c, bb_re)
        bu_im = jnp.einsum('blgp,gnp->blgn', u_c, bb_im)
        a_re = jnp.broadcast_to(ab_re, bu_re.shape)
        a_im = jnp.broadcast_to(ab_im, bu_im.shape)
        pa_re, pa_im, hs_re, hs_im = lax.associative_scan(
            _complex_diag_combine, (a_re, a_im, bu_re, bu_im), axis=1)
        h_re = hs_re + pa_re * h0_re[:, None] - pa_im * h0_im[:, None]
        h_im = hs_im + pa_re * h0_im[:, None] + pa_im * h0_re[:, None]
        y = jnp.einsum('blgn,gpn->blgp', h_re, cr) - jnp.einsum('blgn,gpn->blgp', h_im, ci)
        return (h_re[:, -1], h_im[:, -1]), y

    zeros = jnp.zeros((bsz, SSM_GROUPS, SSM_STATE), f32)
    _, y = lax.scan(chunk_step, (zeros, zeros), uc)
    y = y.transpose(1, 0, 2, 3, 4).reshape(bsz, length, SSM_WIDTH)
    y = y + d_skip.astype(f32) * uf
    g = jax.nn.gelu(y)
    return g * jax.nn.sigmoid(g @ w_glu.astype(f32))


def dilated_attention(q, k, v):
    bsz, length = q.shape[0], q.shape[1]
    n_blocks = length // Q_BLOCK
    scale = 1.0 / math.sqrt(HEAD_DIM)
    k_groups = [k[:, :, g] for g in range(N_DIL)]
    v_groups = [v[:, :, g] for g in range(N_DIL)]

    def one_block(i):
        t = i * Q_BLOCK + jnp.arange(Q_BLOCK)
        q_blk = lax.dynamic_slice_in_dim(q, i * Q_BLOCK, Q_BLOCK, axis=1)
        outs, lses = [], []
        for g, (window, dil) in enumerate(DILATED_GROUPS):
            j = jnp.arange(window // dil + 1)
            idx = t[:, None] - dil * j[None, :]
            valid = idx >= 0
            idxc = jnp.maximum(idx, 0)
            kg = k_groups[g][:, idxc]
            vg = v_groups[g][:, idxc]
            s = jnp.einsum('bqhd,bqjhd->bqhj', q_blk[:, :, g], kg) * scale
            s = jnp.where(valid[None, :, None, :], s, NEG_INF)
            lse = jax.nn.logsumexp(s, axis=-1)
            p = jnp.exp(s - lse[..., None])
            outs.append(jnp.einsum('bqhj,bqjhd->bqhd', p, vg))
            lses.append(lse)
        w = jax.nn.softmax(jnp.stack(lses), axis=0)
        return jnp.einsum('gbqh,gbqhd->bqhd', w, jnp.stack(outs))

    o = lax.map(one_block, jnp.arange(n_blocks))
    return o.transpose(1, 0, 2, 3, 4).reshape(bsz, length, HEADS_PER_GROUP * HEAD_DIM)


def setup_inputs(seed: int = 0) -> dict:
    key = jax.random.key(seed)
    ks = jax.random.split(key, 20)
    f32 = jnp.float32
    x = jax.random.normal(ks[0], (BATCH, SEQ, D_MODEL), f32)
    start = jax.random.randint(ks[1], (BATCH, 1), 0, 1024, dtype=jnp.int32)
    positions = (start + jnp.arange(SEQ, dtype=jnp.int32)[None, :]).astype(jnp.int32)
    norm_w = 1.0 + 0.02 * jax.random.normal(ks[2], (DEPTH, D_MODEL), f32)
    w_in = jax.random.normal(ks[3], (DEPTH, D_MODEL, IN_WIDTH), f32) * D_MODEL ** -0.5
    n_idx = jnp.arange(SSM_STATE, dtype=f32)
    lam_re = -0.5 + 0.01 * jax.random.normal(ks[4], (DEPTH, SSM_GROUPS, SSM_STATE), f32)
    lam_im = math.pi * n_idx + 0.01 * jax.random.normal(ks[5], (DEPTH, SSM_GROUPS, SSM_STATE), f32)
    log_dt = jax.random.uniform(ks[6], (DEPTH, SSM_GROUPS), f32, math.log(DT_MIN), math.log(DT_MAX))
    b_scale = (2.0 * SSM_GROUP) ** -0.5
    b_re = jax.random.normal(ks[7], (DEPTH, SSM_GROUPS, SSM_STATE, SSM_GROUP), f32) * b_scale
    b_im = jax.random.normal(ks[8], (DEPTH, SSM_GROUPS, SSM_STATE, SSM_GROUP), f32) * b_scale
    c_scale = (2.0 * SSM_STATE) ** -0.5
    c_re = jax.random.normal(ks[9], (DEPTH, SSM_GROUPS, SSM_GROUP, SSM_STATE), f32) * c_scale
    c_im = jax.random.normal(ks[10], (DEPTH, SSM_GROUPS, SSM_GROUP, SSM_STATE), f32) * c_scale
    d_skip = jax.random.normal(ks[11], (DEPTH, SSM_WIDTH), f32)
    w_glu = jax.random.normal(ks[12], (DEPTH, SSM_WIDTH, SSM_WIDTH), f32) * SSM_WIDTH ** -0.5
    q_norm_w = 1.0 + 0.02 * jax.random.normal(ks[13], (DEPTH, N_DIL, HEAD_DIM), f32)
    k_norm_w = 1.0 + 0.02 * jax.random.normal(ks[14], (DEPTH, N_DIL, HEAD_DIM), f32)
    w_ssm_out = jax.random.normal(ks[15], (DEPTH, SSM_WIDTH, D_MODEL), f32) * SSM_WIDTH ** -0.5
    w_attn_out = jax.random.normal(ks[16], (DEPTH, ATTN_WIDTH, D_MODEL), f32) * ATTN_WIDTH ** -0.5
    w_o = jax.random.normal(ks[17], (DEPTH, D_MODEL, D_MODEL), f32) * D_MODEL ** -0.5
    return {"x": x, "positions": positions, "norm_w": norm_w, "w_in": w_in,
            "lam_re": lam_re, "lam_im": lam_im, "log_dt": log_dt,
            "b_re": b_re, "b_im": b_im, "c_re": c_re, "c_im": c_im,
            "d_skip": d_skip, "w_glu": w_glu, "q_norm_w": q_norm_w, "k_norm_w": k_norm_w,
            "w_ssm_out": w_ssm_out, "w_attn_out": w_attn_out, "w_o": w_o}


def reference(x, positions, norm_w, w_in, lam_re, lam_im, log_dt, b_re, b_im, c_re, c_im,
              d_skip, w_glu, q_norm_w, k_norm_w, w_ssm_out, w_attn_out, w_o):
    f32 = jnp.float32
    bsz, length, _ = x.shape
    offsets = np.cumsum(SPLITS)[:-1].tolist()
    for layer in range(DEPTH):
        h = rms_norm(x, norm_w[layer])
        proj = h @ w_in[layer].astype(f32)
        u_s, z_s, q, k, v, z_a, r_s, r_a = jnp.split(proj, offsets, axis=-1)

        y_s = s5_branch(u_s, lam_re[layer], lam_im[layer], log_dt[layer], b_re[layer], b_im[layer],
                        c_re[layer], c_im[layer], d_skip[layer], w_glu[layer])
        y_s = (y_s * jax.nn.silu(z_s)) @ w_ssm_out[layer].astype(f32)

        head_shape = (bsz, length, N_DIL, HEADS_PER_GROUP, HEAD_DIM)
        q = rms_norm(q.reshape(head_shape), q_norm_w[layer][None, None, :, None, :])
        k = rms_norm(k.reshape(head_shape), k_norm_w[layer][None, None, :, None, :])
        v = v.reshape(head_shape)
        q = apply_partial_rope(q, positions)
        k = apply_partial_rope(k, positions)
        y_a = dilated_attention(q, k, v)
        y_a = (y_a * jax.nn.silu(z_a)) @ w_attn_out[layer].astype(f32)

        m = jax.nn.sigmoid(r_s) * y_s + jax.nn.sigmoid(r_a) * y_a
        x = (x.astype(f32) + m @ w_o[layer].astype(f32)).astype(x.dtype)
    return x
```

```python
import math
from contextlib import ExitStack

import ml_dtypes
import numpy as np

import concourse.bass as bass
import concourse.mybir as mybir
from concourse.bass_utils import run_bass_kernel_spmd

F32 = mybir.dt.float32
BF16 = mybir.dt.bfloat16
I32 = mybir.dt.int32
AF = mybir.ActivationFunctionType
ALU = mybir.AluOpType
AX = mybir.AxisListType

D = 1024
DT = 8
NCORES = 8
EPS = 1e-6
ENGS = ["tensor", "vector", "scalar", "gpsimd", "sync"]
PI = math.pi

EVALS = [float(7 - i) for i in range(16)] + [float(i) for i in range(1, 9)]
NE = len(EVALS)


import os as _os
STRICT = bool(int(_os.environ.get("STRICT_SYNC", "1")))


class Prog:
    def __init__(self, nc, es):
        self.nc = nc
        self.ops = []
        self.stream = {e: [] for e in ENGS}
        self.res_w = {}
        self.res_r = {}
        self.last = {e: None for e in ENGS}
        self.sems = {e: es.enter_context(nc.semaphore(f"s_{e}")) for e in ENGS[:4]}
        self.qpool = {"sync": list(range(0, 16)), "gpsimd": list(range(16, 24)), "scalar": list(range(24, 32))}
        self.dsems = [es.enter_context(nc.semaphore(f"dq{i}")) for i in range(32)]
        self.qn = {"sync": 0, "gpsimd": 0, "scalar": 0}
        self.dma_prev = {}
        self.dma_out = []

    def op(self, eng, fn, r=(), w=(), dma=False, extra=()):
        oid = len(self.ops)
        deps = set(extra)
        psr = [k for k in r if k.startswith("ps")]
        w = list(w) + psr
        r = [k for k in r if not k.startswith("ps")]
        for k in r:
            x = self.res_w.get(k)
            if x is not None:
                deps.add(x)
        for k in w:
            x = self.res_w.get(k)
            if x is not None:
                ox = self.ops[x]
                if STRICT or dma or ox["dma"] or ox["eng"] != eng or k in psr:
                    deps.add(x)
            rr = self.res_r.get(k)
            if rr:
                for ek, rid in rr.items():
                    if STRICT or dma or ek != eng:
                        deps.add(rid)
        o = dict(id=oid, eng=eng, fn=fn, deps=deps, dma=dma, sig=False)
        if dma:
            pool = self.qpool[eng]
            idx = self.qn[eng]
            self.qn[eng] += 1
            o["dsem"] = pool[idx % len(pool)]
            o["dval"] = 16 * (idx // len(pool) + 1)
            prev = self.dma_prev.get(o["dsem"])
            if prev is not None:
                deps.add(prev)
            self.dma_prev[o["dsem"]] = oid
            self.dma_out.append(oid)
        self.ops.append(o)
        self.stream[eng].append(oid)
        for k in r:
            d = self.res_r.setdefault(k, {})
            d[("dma", oid) if dma else eng] = oid
        for k in w:
            self.res_w[k] = oid
            self.res_r[k] = {}
        self.last[eng] = oid
        return oid

    def barrier(self):
        deps = set(x for x in self.last.values() if x is not None)
        deps.update(self.dma_out)
        self.dma_out = []
        for e in ENGS:
            self.op(e, None, extra=deps)
        self.res_w = {}
        self.res_r = {}

    def emit(self):
        ops = self.ops
        for o in ops:
            keep = set()
            for d in o["deps"]:
                od = ops[d]
                if od["fn"] is None:
                    continue
                if (not od["dma"]) and (not o["dma"]) and od["eng"] == o["eng"] == "tensor":
                    continue
                keep.add(d)
                if not od["dma"]:
                    od["sig"] = True
            o["deps"] = keep
        cnt = {e: 0 for e in ENGS}
        for o in ops:
            if o["sig"]:
                cnt[o["eng"]] += 1
                o["cnt"] = cnt[o["eng"]]
        nc = self.nc
        with nc.Block() as block:
            for e in ENGS:
                def body(eng, e=e):
                    waited = {}
                    for oid in self.stream[e]:
                        o = ops[oid]
                        need = {}
                        for d in o["deps"]:
                            od = ops[d]
                            if od["dma"]:
                                key = ("d", od["dsem"])
                                val = od["dval"]
                            else:
                                key = ("e", od["eng"])
                                val = od["cnt"]
                            if val > need.get(key, 0):
                                need[key] = val
                        for key, val in need.items():
                            if waited.get(key, 0) >= val:
                                continue
                            waited[key] = val
                            sem = self.dsems[key[1]] if key[0] == "d" else self.sems[key[1]]
                            eng.wait_ge(sem, val)
                        if o["fn"] is None:
                            continue
                        ins = o["fn"](eng)
                        if o["dma"]:
                            ins.then_inc(self.dsems[o["dsem"]], 16)
                        elif o["sig"]:
                            ins.then_inc(self.sems[e], 1)
                getattr(block, e)(body)


def sap(t, part0, nparts, off, dims):
    full = t[:]
    pstride = full.ap[0][0]
    return bass.AP(full.tensor, part0 * pstride + off, [[pstride, nparts]] + [list(d) for d in dims])


import os
CPENG = os.environ.get("CPENG", "vector,scalar").split(",")
SKIPA = bool(int(os.environ.get("SKIPA", "0")))
NHEADS = int(os.environ.get("NHEADS", "8"))


def build(L, NSEQ, stop_after=None, dbg=()):
    NT = L // 128
    NBLK = L // 1024
    nc = bass.Bass("TRN2", target_bir_lowering=False)
    dram = {}

    def din(name, shape, dt):
        dram[name] = nc.dram_tensor(name, list(shape), dt, kind="ExternalInput").ap()
        return dram[name]

    x_d = din("x", [NSEQ, L, D], F32)
    pos_d = din("pos", [NSEQ, NT, 128], I32)
    normw_d = din("norm_w", [1, D], F32)
    win_d = din("w_in", [D, 9216], F32)
    lamre_d = din("lam_re", [32, 128], F32)
    lamim_d = din("lam_im", [32, 128], F32)
    logdt_d = din("log_dt", [32, 2], F32)
    bre_d = din("b_re", [64 * 64 * 16], F32)
    bim_d = din("b_im", [64 * 64 * 16], F32)
    cre_d = din("c_re", [64, 16, 64], F32)
    cim_d = din("c_im", [64, 16, 64], F32)
    dskip_d = din("d_skip", [64, 16], F32)
    wglu_d = din("w_glu", [D, D], F32)
    qnw_d = din("q_norm_w", [1, 192], F32)
    knw_d = din("k_norm_w", [1, 192], F32)
    wso_d = din("w_ssm_out", [D, D], F32)
    wao_d = din("w_attn_out", [512, D], F32)
    wo_d = din("w_o", [D, D], F32)
    identb_d = din("ident_bf", [128, 128], BF16)
    identf_d = din("ident_f", [128, 128], F32)
    amask_d = din("amask", [128, 512], BF16)
    trit_d = din("trit", [128, 128], F32)
    evals_d = din("evals", [128, NE], F32)
    invf_d = din("invf", [128, 8], F32)
    out_d = nc.dram_tensor("out", [NSEQ, L, D], F32, kind="ExternalOutput").ap()
    dbg_d = {}
    for name, shape, dt in dbg:
        dbg_d[name] = nc.dram_tensor(name, list(shape), dt, kind="ExternalOutput").ap()

    Wb_in = nc.dram_tensor("Wb_in", [D, 9216], BF16).ap()
    Wb_glu = nc.dram_tensor("Wb_glu", [D, D], BF16).ap()
    Wb_so = nc.dram_tensor("Wb_so", [D, D], BF16).ap()
    Wb_ao = nc.dram_tensor("Wb_ao", [512, D], BF16).ap()
    Wb_o = nc.dram_tensor("Wb_o", [D, D], BF16).ap()
    NSL = 12
    SSMW_d = nc.dram_tensor("SSMW", [128, 32 * NSL * 128], BF16).ap()
    A_d = nc.dram_tensor("A_scr", [NSEQ, 512, L], BF16).ap()

    es = ExitStack()
    P = Prog(nc, es)
    uid = [0]

    def A(name, shape, dt):
        uid[0] += 1
        return es_phase[0].enter_context(nc.sbuf_tensor("%s_%d" % (name, uid[0]), list(shape), dt))

    def AG(name, shape, dt):
        return es.enter_context(nc.sbuf_tensor(name, list(shape), dt))

    def psum(name, shape, dt=F32):
        uid[0] += 1
        return es_phase[0].enter_context(nc.psum_tensor("%s_%d" % (name, uid[0]), list(shape), dt))

    es_phase = [es]

    def dma(out, in_, r=(), w=(), q="sync", **kw):
        return P.op(q, lambda e: e.dma_start(out=out, in_=in_, **kw), r=r, w=w, dma=True)

    def dmaP(out, in_, r=(), w=(), **kw):
        return dma(out, in_, r=r, w=w, q="scalar", **kw)

    def mm(out, lhsT, rhs, start, stop, r=(), w=()):
        return P.op("tensor", lambda e: e.matmul(out, lhsT, rhs, start=start, stop=stop), r=r, w=w)

    def tp(out, in_, ident, r=(), w=()):
        return P.op("tensor", lambda e: e.transpose(out, in_, ident), r=r, w=w)

    def act(out, in_, func, r=(), w=(), bias=0.0, scale=1.0, accum_out=None):
        if accum_out is None:
            return P.op("scalar", lambda e: e.activation(out, in_, func, bias=bias, scale=scale), r=r, w=w)
        return P.op("scalar", lambda e: e.activation(out, in_, func, bias=bias, scale=scale, accum_out=accum_out), r=r, w=w)

    def tt(out, in0, in1, op, r=(), w=(), eng="vector"):
        return P.op(eng, lambda e: e.tensor_tensor(out, in0, in1, op), r=r, w=w)

    def ts(out, in0, s1, s2, op0, op1=None, r=(), w=(), eng="vector"):
        if op1 is None:
            return P.op(eng, lambda e: e.tensor_scalar(out, in0, s1, None, op0), r=r, w=w)
        return P.op(eng, lambda e: e.tensor_scalar(out, in0, s1, s2, op0, op1), r=r, w=w)

    def stt(out, in0, scalar, in1, op0, op1, r=(), w=()):
        return P.op("vector", lambda e: e.scalar_tensor_tensor(out, in0, scalar, in1, op0, op1), r=r, w=w)

    def cp(out, in_, r=(), w=(), eng="vector"):
        if eng == "scalar":
            return P.op("scalar", lambda e: e.copy(out, in_), r=r, w=w)
        return P.op(eng, lambda e: e.tensor_copy(out, in_), r=r, w=w)

    def memset(ap, val, w=(), eng="vector"):
        return P.op(eng, lambda e: e.memset(ap, val), w=w)

    def recip(out, in_, r=(), w=()):
        return P.op("vector", lambda e: e.reciprocal(out, in_), r=r, w=w)

    def dbg_dump(name, src_ap, keys=()):
        if name in dbg_d:
            P.barrier()
            dma(dbg_d[name], src_ap, q="sync")
            P.barrier()

    def range_reduce_sincos(ang, sin_out, cos_out, shape, tmpname):
        n = int(np.prod(shape[1:]))
        ki = A(tmpname + "_ki", [128, n], I32)
        kf = A(tmpname + "_kf", [128, n], F32)
        r_ = A(tmpname + "_r", [128, n], F32)
        y = A(tmpname + "_y", [128, n], F32)
        t1 = A(tmpname + "_t1", [128, n], F32)
        K = tmpname
        angf = ang
        ts(ki[:], angf, 1.0 / (2 * PI), None, ALU.mult, r=[K + "ang"], w=[K + "ki"])
        cp(kf[:], ki[:], r=[K + "ki"], w=[K + "kf"])
        stt(r_[:], kf[:], -2 * PI, angf, ALU.mult, ALU.add, r=[K + "kf", K + "ang"], w=[K + "r"])
        for shift, dst in ((0.0, sin_out), (PI / 2, cos_out)):
            ts(y[:], r_[:], shift, None, ALU.add, r=[K + "r"], w=[K + "y"])
            ts(t1[:], y[:], PI, -2 * PI, ALU.is_gt, ALU.mult, r=[K + "y"], w=[K + "t1"])
            tt(y[:], y[:], t1[:], ALU.add, r=[K + "y", K + "t1"], w=[K + "y"])
            ts(t1[:], y[:], -PI, 2 * PI, ALU.is_lt, ALU.mult, r=[K + "y"], w=[K + "t1"])
            tt(y[:], y[:], t1[:], ALU.add, r=[K + "y", K + "t1"], w=[K + "y"])
            ts(y[:], y[:], PI, -PI, ALU.min, ALU.max, r=[K + "y"], w=[K + "y"])
            act(dst, y[:], AF.Sin, r=[K + "y"], w=[K + "out" + str(shift)])

    ident_b = AG("sb_ident_b", [128, 128], BF16)
    ident_f = AG("sb_ident_f", [128, 128], F32)
    amask = AG("sb_amask", [128, 512], BF16)
    qkw_rep = AG("qkw_rep", [128, 6, 64], F32)
    invf = AG("sb_invf", [128, 8], F32)
    a8 = AG("a8", [128, 2, 2, 32], F32)
    ones_f = AG("ones_f", [128, 64], F32)
    dma(ident_b[:], identb_d, w=["c0"])
    dma(ident_f[:], identf_d, w=["c1"])
    dma(amask[:], amask_d, w=["c2"])
    for g in range(3):
        dma(qkw_rep[:, g, :], qnw_d[:, g * 64:(g + 1) * 64].partition_broadcast(128), w=["c4%d" % g])
        dma(qkw_rep[:, 3 + g, :], knw_d[:, g * 64:(g + 1) * 64].partition_broadcast(128), w=["c5%d" % g])
    dma(invf[:], invf_d, w=["c6"])
    memset(ones_f[:], 1.0, w=["c7"])
    EPS_AP = AG("eps_ap", [128, 1], F32)
    memset(EPS_AP[:], EPS, w=["c8"])
    P.barrier()
    if stop_after == "C":
        return finish(nc, P, es, out_d, None)

    es_phaseP = ExitStack()
    es_phase[0] = es_phaseP
    if True:
        raw = A("p_raw", [32, 3, 128], F32)
        ldt = A("p_ldt", [32, 2], F32)
        dmaP(raw[:, 0, :], lamre_d, w=["raw0"])
        dmaP(raw[:, 1, :], lamim_d, w=["raw1"])
        dmaP(ldt[:], logdt_d, w=["ldt"])
        cp(raw[:, 2, :].rearrange("p (a b) -> p a b", a=2), ldt[:].unsqueeze(2).to_broadcast([32, 2, 64]),
           r=["ldt"], w=["raw2"])
        ps_t = psum("p_ps_t", [128, 3, 32], F32)
        for j in range(3):
            tp(ps_t[:, j, :], raw[:, j, :], ident_f[0:32, 0:32], r=["raw%d" % j], w=["ps_t"])
        lam = A("p_lam", [128, 3, 32], F32)
        cp(lam[:], ps_t[:], r=["ps_t"], w=["lam"])
        if stop_after == "P1":
            return finish(nc, P, es, out_d, es_phase[0])

        dtt = A("p_dt", [128, 32], F32)
        act(dtt[:], lam[:, 2, :], AF.Exp, r=["lam"], w=["dt"])
        lrdt = A("p_lrdt", [128, 32], F32)
        lidt = A("p_lidt", [128, 32], F32)
        tt(lrdt[:], lam[:, 0, :], dtt[:], ALU.mult, r=["lam", "dt"], w=["lrdt"])
        tt(lidt[:], lam[:, 1, :], dtt[:], ALU.mult, r=["lam", "dt"], w=["lidt"])
        ev = A("p_ev", [128, NE], F32)
        dmaP(ev[:], evals_d, w=["ev"])
        marg = A("p_marg", [128, 32, NE], F32)
        ang = A("p_ang", [128, 32, NE], F32)
        tt(marg[:], lrdt[:].unsqueeze(2).to_broadcast([128, 32, NE]),
           ev[:].unsqueeze(1).to_broadcast([128, 32, NE]), ALU.mult, r=["lrdt", "ev"], w=["marg"])
        tt(ang[:], lidt[:].unsqueeze(2).to_broadcast([128, 32, NE]),
           ev[:].unsqueeze(1).to_broadcast([128, 32, NE]), ALU.mult, r=["lidt", "ev"], w=["pPang"])
        mag = A("p_mag", [128, 32, NE], F32)
        act(mag[:], marg[:], AF.Exp, r=["marg"], w=["mag"])
        if stop_after == "P2":
            return finish(nc, P, es, out_d, es_phase[0])

        sn = A("p_sin", [128, 32 * NE], F32)
        cs = A("p_cos", [128, 32 * NE], F32)
        range_reduce_sincos(ang[:].rearrange("p a b -> p (a b)"), sn[:], cs[:], [128, 32 * NE], "pP")
        PRE = A("p_PRE", [128, 32, NE], F32)
        PIM = A("p_PIM", [128, 32, NE], F32)
        tt(PRE[:].rearrange("p a b -> p (a b)"), mag[:].rearrange("p a b -> p (a b)"), cs[:], ALU.mult,
           r=["mag", "pPout" + str(PI / 2)], w=["PRE"])
        tt(PIM[:].rearrange("p a b -> p (a b)"), mag[:].rearrange("p a b -> p (a b)"), sn[:], ALU.mult,
           r=["mag", "pPout0.0"], w=["PIM"])
        cp(a8[:, 0, 0, :], PRE[:, :, 23], r=["PRE"], w=["a8a"])
        cp(a8[:, 0, 1, :], PRE[:, :, 23], r=["PRE"], w=["a8b"])
        ts(a8[:, 1, 0, :], PIM[:, :, 23], -1.0, None, ALU.mult, r=["PIM"], w=["a8c"])
        cp(a8[:, 1, 1, :], PIM[:, :, 23], r=["PIM"], w=["a8d"])
        if stop_after == "P3":
            return finish(nc, P, es, out_d, es_phase[0])

        den = A("p_den", [128, 32], F32)
        t0 = A("p_t0", [128, 32], F32)
        t1_ = A("p_t1", [128, 32], F32)
        nr = A("p_nr", [128, 32], F32)
        fre = A("p_fre", [128, 32], F32)
        fim = A("p_fim", [128, 32], F32)
        LR = lam[:, 0, :]
        LI = lam[:, 1, :]
        tt(den[:], LR, LR, ALU.mult, r=["lam"], w=["den"])
        tt(t0[:], LI, LI, ALU.mult, r=["lam"], w=["t0"])
        tt(den[:], den[:], t0[:], ALU.add, r=["den", "t0"], w=["den"])
        recip(den[:], den[:], r=["den"], w=["den"])
        ts(nr[:], PRE[:, :, 16], -1.0, None, ALU.add, r=["PRE"], w=["nr"])
        tt(t0[:], nr[:], LR, ALU.mult, r=["nr", "lam"], w=["t0"])
        tt(t1_[:], PIM[:, :, 16], LI, ALU.mult, r=["PIM", "lam"], w=["t1"])
        tt(t0[:], t0[:], t1_[:], ALU.add, r=["t0", "t1"], w=["t0"])
        tt(fre[:], t0[:], den[:], ALU.mult, r=["t0", "den"], w=["fre"])
        tt(t0[:], PIM[:, :, 16], LR, ALU.mult, r=["PIM", "lam"], w=["t0"])
        tt(t1_[:], nr[:], LI, ALU.mult, r=["nr", "lam"], w=["t1"])
        tt(t0[:], t0[:], t1_[:], ALU.subtract, r=["t0", "t1"], w=["t0"])
        tt(fim[:], t0[:], den[:], ALU.mult, r=["t0", "den"], w=["fim"])
        Bre = A("p_Bre", [128, 32, 16], F32)
        Bim = A("p_Bim", [128, 32, 16], F32)
        for q4 in range(4):
            for (src, dstt, kk) in ((bre_d, Bre, "Bre"), (bim_d, Bim, "Bim")):
                s_ap = bass.AP(src.tensor, q4 * 8 * 2048, [[16, 128], [2048, 8], [1, 16]])
                dmaP(dstt[:, q4 * 8:(q4 + 1) * 8, :], s_ap, w=[kk + str(q4)])
        BBre = A("p_BBre", [128, 32, 16], F32)
        BBim = A("p_BBim", [128, 32, 16], F32)
        tb0 = A("p_tb0", [128, 32, 16], F32)
        Bk = ["Bre%d" % i for i in range(4)] + ["Bim%d" % i for i in range(4)]
        fre_b = fre[:].unsqueeze(2).to_broadcast([128, 32, 16])
        fim_b = fim[:].unsqueeze(2).to_broadcast([128, 32, 16])
        tt(BBre[:], Bre[:], fre_b, ALU.mult, r=Bk + ["fre"], w=["BBre"])
        tt(tb0[:], Bim[:], fim_b, ALU.mult, r=Bk + ["fim"], w=["tb0"])
        tt(BBre[:], BBre[:], tb0[:], ALU.subtract, r=["BBre", "tb0"], w=["BBre"])
        tt(BBim[:], Bim[:], fre_b, ALU.mult, r=Bk + ["fre"], w=["BBim"])
        tt(tb0[:], Bre[:], fim_b, ALU.mult, r=Bk + ["fim"], w=["tb0"])
        tt(BBim[:], BBim[:], tb0[:], ALU.add, r=["BBim", "tb0"], w=["BBim"])
        if stop_after == "P4":
            return finish(nc, P, es, out_d, es_phase[0])

        Cre = A("p_Cre", [128, 32, 16], F32)
        Cim = A("p_Cim", [128, 32, 16], F32)
        cst = A("p_cst", [128, 2, 2, 64], F32)
        ps_c = psum("p_ps_c", [128, 2, 128], F32)
        for k4 in range(4):
            for gq in range(8):
                gp = k4 * 8 + gq
                for ri, src in ((0, cre_d), (1, cim_d)):
                    s_ap = bass.AP(src.tensor, (2 * gp) * 1024, [[64, 16], [1024, 2], [1, 64]])
                    dmaP(cst[gq * 16:(gq + 1) * 16, ri, :, :], s_ap, w=["cst%d_%d" % (gq, ri)],
                        r=[])
            ck = ["cst%d_%d" % (gq, ri) for gq in range(8) for ri in range(2)]
            for ri in range(2):
                tp(ps_c[:, ri, :], cst[:, ri, :, :].rearrange("p a b -> p (a b)"), ident_f[:], r=ck, w=["ps_c"])
            cp(Cre[:, k4 * 8:(k4 + 1) * 8, :], ps_c[:, 0, :].rearrange("p (a b) -> p a b", a=8), r=["ps_c"], w=["Cre%d" % k4])
            cp(Cim[:, k4 * 8:(k4 + 1) * 8, :], ps_c[:, 1, :].rearrange("p (a b) -> p a b", a=8), r=["ps_c"], w=["Cim%d" % k4])
        Ck = ["Cre%d" % i for i in range(4)] + ["Cim%d" % i for i in range(4)]
        if stop_after == "P5":
            return finish(nc, P, es, out_d, es_phase[0])

        dsk0 = A("p_dsk0", [64, 16], F32)
        dsk1 = A("p_dsk1", [64, 8, 16], F32)
        dmaP(dsk0[:], dskip_d, w=["dsk0"])
        cp(dsk1[:], dsk0[:].unsqueeze(1).to_broadcast([64, 8, 16]), r=["dsk0"], w=["dsk1"])
        ps_d = psum("p_ps_d", [128, 64], F32)
        tp(ps_d[:], dsk1[:].rearrange("p a b -> p (a b)"), ident_f[0:64, 0:64], r=["dsk1"], w=["ps_d"])
        DSK = A("p_DSK", [128, 64], F32)
        cp(DSK[:], ps_d[:], r=["ps_d"], w=["DSK"])
        trit = A("p_trit", [128, 128], F32)
        dmaP(trit[:], trit_d, w=["trit"])
        if stop_after == "P6":
            return finish(nc, P, es, out_d, es_phase[0])


        SW = A("p_SW", [128, 8, NSL, 128], BF16)
        GR = A("p_GR", [128, 8, 16, 16], F32)
        GI = A("p_GI", [128, 8, 16, 16], F32)
        GT = A("p_GT", [128, 8, 16, 16], F32)
        GRb = A("p_GRb", [128, 8, 2, 128], BF16)
        WCf = A("p_WCf", [128, 8, 2, 2, 128], F32)
        WCt = A("p_WCt", [128, 8, 8, 16], F32)
        WCt2 = A("p_WCt2", [128, 8, 8, 16], F32)
        ps_w = psum("p_ps_w", [128, 2, 128], BF16)
        ps_z = psum("p_ps_z", [128, 2, 128], F32)
        tzt = A("p_tzt", [128, 2, 128], F32)
        memset(WCf[:], 0.0, w=["WCf"])
        for k4 in range(4):
            gsl = slice(k4 * 8, (k4 + 1) * 8)
            sh = [128, 8, 16, 16]
            pre_a = PRE[:, gsl, 0:16].unsqueeze(3).to_broadcast(sh)
            pim_a = PIM[:, gsl, 0:16].unsqueeze(3).to_broadcast(sh)
            bbr = BBre[:, gsl, :].unsqueeze(2).to_broadcast(sh)
            bbi = BBim[:, gsl, :].unsqueeze(2).to_broadcast(sh)
            tt(GR[:], pre_a, bbr, ALU.mult, r=["PRE", "BBre"], w=["GR"])
            tt(GT[:], pim_a, bbi, ALU.mult, r=["PIM", "BBim"], w=["GT"])
            tt(GR[:], GR[:], GT[:], ALU.subtract, r=["GR", "GT"], w=["GR"])
            tt(GI[:], pre_a, bbi, ALU.mult, r=["PRE", "BBim"], w=["GI"])
            tt(GT[:], pim_a, bbr, ALU.mult, r=["PIM", "BBre"], w=["GT"])
            tt(GI[:], GI[:], GT[:], ALU.add, r=["GI", "GT"], w=["GI"])
            cp(GRb[:, :, 0, :], GR[:, :, 0:8, :].rearrange("p g a b -> p g (a b)"), r=["GR"], w=["GRb0"])
            cp(GRb[:, :, 1, :], GI[:, :, 0:8, :].rearrange("p g a b -> p g (a b)"), r=["GI"], w=["GRb1"])
            if stop_after == "P7":
                return finish(nc, P, es, out_d, es_phase[0])

            sh2 = [128, 8, 8, 16]
            pre_c = PRE[:, gsl, 16:24].unsqueeze(3).to_broadcast(sh2)
            pim_c = PIM[:, gsl, 16:24].unsqueeze(3).to_broadcast(sh2)
            crb = Cre[:, gsl, :].unsqueeze(2).to_broadcast(sh2)
            cib = Cim[:, gsl, :].unsqueeze(2).to_broadcast(sh2)
            tt(WCt[:], crb, pre_c, ALU.mult, r=Ck + ["PRE"], w=["WCt"])
            tt(WCt2[:], cib, pim_c, ALU.mult, r=Ck + ["PIM"], w=["WCt2"])
            tt(WCt[:], WCt[:], WCt2[:], ALU.subtract, r=["WCt", "WCt2"], w=["WCt"])
            for hh in range(2):
                rows = slice(hh * 64, (hh + 1) * 64)
                cp(WCf[rows, :, 0, hh, :], WCt[rows, :, :, :].rearrange("p g a b -> p g (a b)"),
                   r=["WCt", "WCf"], w=["WCf0%d" % hh])
            tt(WCt[:], crb, pim_c, ALU.mult, r=Ck + ["PIM", "WCf00", "WCf01"], w=["WCt"])
            tt(WCt2[:], cib, pre_c, ALU.mult, r=Ck + ["PRE"], w=["WCt2"])
            tt(WCt[:], WCt[:], WCt2[:], ALU.add, r=["WCt", "WCt2"], w=["WCt"])
            for hh in range(2):
                rows = slice(hh * 64, (hh + 1) * 64)
                ts(WCf[rows, :, 1, hh, :], WCt[rows, :, :, :].rearrange("p g a b -> p g (a b)"), -1.0, None,
                   ALU.mult, r=["WCt", "WCf"], w=["WCf1%d" % hh])
            WCk = ["WCf00", "WCf01", "WCf10", "WCf11"]
            if stop_after == "P8":
                return finish(nc, P, es, out_d, es_phase[0])

            for ri in range(2):
                for hh in range(2):
                    cp(SW[:, :, 4 + 2 * ri + hh, :], WCf[:, :, ri, hh, :], r=WCk, w=["SW%d" % (4 + 2 * ri + hh)],
                       eng="scalar")
            memset(SW[:, :, 0:4, :], 0.0, w=["SW0", "SW1", "SW2", "SW3"])
            if stop_after == "P9":
                return finish(nc, P, es, out_d, es_phase[0])

            for gq in range(8):
                for ri in range(2):
                    tp(ps_w[:, ri, :], GRb[:, gq, ri, :], ident_b[:], r=["GRb%d" % ri], w=["ps_w"])
                if stop_after == "P9b":
                    return finish(nc, P, es, out_d, es_phase[0])
                for ri in range(2):
                    for hh in range(2):
                        cs_ = slice(hh * 64, (hh + 1) * 64)
                        cp(SW[:, gq, 2 * ri + hh, cs_], ps_w[:, ri, cs_], r=["ps_w"],
                           w=["SW%d" % (2 * ri + hh)], eng=CPENG[hh])
                if stop_after == "P10":
                    return finish(nc, P, es, out_d, es_phase[0])
                for hh in range(2):
                    mm(ps_z[:, hh, :], GR[:, gq, 8:16, :].rearrange("p a b -> p (a b)"), WCf[:, gq, 0, hh, :],
                       True, False, r=["GR"] + WCk, w=["ps_z"])
                    mm(ps_z[:, hh, :], GI[:, gq, 8:16, :].rearrange("p a b -> p (a b)"), WCf[:, gq, 1, hh, :],
                       False, True, r=["GI"] + WCk, w=["ps_z"])
                    g = 2 * (k4 * 8 + gq) + hh
                    tt(tzt[:, hh, :], ps_z[:, hh, :], trit[:], ALU.mult, r=["ps_z", "trit"], w=["tzt%d" % hh])
                    stt(SW[:, gq, 8 + hh, :], ident_f[:], DSK[:, g:g + 1], tzt[:, hh, :], ALU.mult, ALU.add,
                        r=["tzt%d" % hh, "DSK"], w=["SW%d" % (8 + hh)])
            if stop_after == "P11":
                return finish(nc, P, es, out_d, es_phase[0])
            memset(SW[:, :, 10:12, :], 0.0, w=["SW10"])
            allk = ["SW%d" % i for i in range(11)]
            dmaP(SSMW_d[:, k4 * 8 * NSL * 128:(k4 + 1) * 8 * NSL * 128], SW[:].rearrange("p a b c -> p (a b c)"),
                r=allk)
        if "dbg_PRE" in dbg_d:
            dbg_dump("dbg_PRE", PRE[:].rearrange("p a b -> p (a b)"))
        if "dbg_PIM" in dbg_d:
            dbg_dump("dbg_PIM", PIM[:].rearrange("p a b -> p (a b)"))
    if stop_after == "P":
        if "dbg_SSMW" in dbg_d:
            dma(dbg_d["dbg_SSMW"], SSMW_d)
        return finish(nc, P, es, out_d, None)


    es_phaseW = ExitStack()
    es_phase[0] = es_phaseW
    if True:
        CW = 2304
        stg = [A("w_stg%d" % i, [128, CW], F32) for i in range(2)]
        stb = [A("w_stb%d" % i, [128, CW], BF16) for i in range(2)]
        jobs = []
        for rt in range(8):
            for c in range(4):
                jobs.append((win_d[rt * 128:(rt + 1) * 128, c * CW:(c + 1) * CW],
                             Wb_in[rt * 128:(rt + 1) * 128, c * CW:(c + 1) * CW], CW))
        for src, dst, nr in ((wglu_d, Wb_glu, 8), (wso_d, Wb_so, 8), (wao_d, Wb_ao, 4), (wo_d, Wb_o, 8)):
            for rt in range(nr):
                jobs.append((src[rt * 128:(rt + 1) * 128, :], dst[rt * 128:(rt + 1) * 128, :], D))
        for i, (src, dst, n) in enumerate(jobs):
            b = i % 2
            dma(stg[b][:, 0:n], src, w=["wstg%d" % b])
            cp(stb[b][:, 0:n], stg[b][:, 0:n], r=["wstg%d" % b], w=["wstb%d" % b],
               eng="gpsimd")
            dma(dst, stb[b][:, 0:n], r=["wstb%d" % b], q="gpsimd")
    P.barrier()
    es_phaseW.close()
    es_phaseP.close()

    hT = AG("hT", [128, DT, L], BF16)
    cos2 = AG("cos2", [128, NT, 16], F32)
    sin2 = AG("sin2", [128, NT, 16], F32)

    for sq in range(NSEQ):
        es_phase[0] = ExitStack()
        cosT = A("r_cosT", [128, NT, 8], F32)
        sinT = A("r_sinT", [128, NT, 8], F32)
        posi = A("r_posi", [NT, 128], I32)
        posf = A("r_posf", [NT, 128], F32)
        dma(posi[:], pos_d[sq], w=["posi"])
        cp(posf[:], posi[:], r=["posi"], w=["posf"])
        ps_p = psum("r_ps_p", [128, NT], F32)
        tp(ps_p[:], posf[:], ident_f[0:NT, 0:NT], r=["posf"], w=["ps_p"])
        posT = A("r_posT", [128, NT], F32)
        cp(posT[:], ps_p[:], r=["ps_p"], w=["posT"])
        rang = A("r_ang", [128, NT, 8], F32)
        tt(rang[:], posT[:].unsqueeze(2).to_broadcast([128, NT, 8]),
           invf[:].unsqueeze(1).to_broadcast([128, NT, 8]), ALU.mult, r=["posT"], w=["rRang"])
        range_reduce_sincos(rang[:].rearrange("p a b -> p (a b)"), sinT[:].rearrange("p a b -> p (a b)"),
                            cosT[:].rearrange("p a b -> p (a b)"), [128, NT * 8], "rR%d" % sq if False else "rR")
        cp(cos2[:, :, 0:8], cosT[:], r=["rRout" + str(PI / 2)], w=["cos2a"])
        cp(cos2[:, :, 8:16], cosT[:], r=["rRout" + str(PI / 2)], w=["cos2b"])
        ts(sin2[:, :, 0:8], sinT[:], -1.0, None, ALU.mult, r=["rRout0.0"], w=["sin2a"])
        cp(sin2[:, :, 8:16], sinT[:], r=["rRout0.0"], w=["sin2b"])
        P.barrier()
        es_phase[0].close()
        es_phase[0] = ExitStack()
        normw_rep = A("n_normw_rep", [128, D], F32)
        dma(normw_rep[:], normw_d.partition_broadcast(128), w=["normw"])
        xt = [A("n_x%d" % i, [128, D], F32) for i in range(3)]
        xsq = A("n_xsq", [128, D], F32)
        hb = [A("n_hb%d" % i, [128, D], BF16) for i in range(3)]
        ssq = [A("n_ss%d" % i, [128, 1], F32) for i in range(3)]
        rst = [A("n_rs%d" % i, [128, 1], F32) for i in range(3)]
        ps_h = [psum("n_ps_h%d" % i, [128, DT, 128], BF16) for i in range(2)]

        def n_s1(t):
            b = t % 3
            dma(xt[b][:], x_d[sq, t * 128:(t + 1) * 128, :], w=["xt%d" % b])
            act(xsq[:], xt[b][:], AF.Square, r=["xt%d" % b], w=["xsq", "ss%d" % b], accum_out=ssq[b][:])
            act(rst[b][:], ssq[b][:], AF.Sqrt, r=["ss%d" % b], w=["rs%d" % b], bias=EPS_AP[:], scale=1.0 / D)
            recip(rst[b][:], rst[b][:], r=["rs%d" % b], w=["rs%d" % b])
            stt(hb[b][:], xt[b][:], rst[b][:, 0:1], normw_rep[:], ALU.mult, ALU.mult,
                r=["xt%d" % b, "rs%d" % b, "normw"], w=["hb%d" % b])

        def n_s2(t):
            b = t % 3
            pb = t % 2
            for dt_ in range(DT):
                tp(ps_h[pb][:, dt_, :], hb[b][:, dt_ * 128:(dt_ + 1) * 128], ident_b[:], r=["hb%d" % b],
                   w=["ps_h%d" % pb])
            cp(hT[:, :, t * 128:(t + 1) * 128], ps_h[pb][:], r=["ps_h%d" % pb], w=["hT"],
               eng="scalar" if t % 2 else "vector")

        for i in range(NT + 1):
            if i < NT:
                n_s1(i)
            if i >= 1:
                n_s2(i - 1)
        P.barrier()
        es_phase[0].close()
        if stop_after == "N":
            for nm, src_ in (("dbg_hT", hT[:].rearrange("p a b -> p (a b)")),
                             ("dbg_cos", cosT[:].rearrange("p a b -> p (a b)")),
                             ("dbg_sin", sinT[:].rearrange("p a b -> p (a b)"))):
                if nm in dbg_d:
                    dma(dbg_d[nm], src_)
            return finish(nc, P, es, out_d, None)

        es_phase[0] = ExitStack()
        DILS = (1, 4, 16)
        NB = L // 128
        Wqk = A("a_Wqk", [128, DT, 384], BF16)
        Wv = A("a_Wv", [128, DT, 192], BF16)
        sq_s = [A("a_sq%d" % i, [128, 6, 64], F32) for i in range(3)]
        ss6 = [A("a_ss6%d" % i, [128, 6], F32) for i in range(3)]
        rstd6 = [A("a_rstd6%d" % i, [128, 6], F32) for i in range(3)]
        qn = [A("a_qn%d" % i, [128, 6, 64], F32) for i in range(3)]
        qb = [A("a_qb%d" % i, [128, 6, 64], BF16) for i in range(3)]
        rr4 = [[A("a_r%d_%d" % (j, i), [128, 6, 16], F32) for j in range(2)] for i in range(3)]
        QK01 = A("a_QK01", [128, 2, L], BF16)
        QK2 = A("a_QK2", [64, 2, L], BF16)
        Vg = [A("a_V%d" % g, [128, NB, 65], BF16) for g in range(3)]
        Oacc = A("a_Oacc", [65, L], F32)
        Pm = [A("a_Pm%d" % i, [128, 512], BF16) for i in range(4)]
        yA = A("a_yA", [64, L], BF16)
        ps_T = [psum("a_ps_T%d" % i, [128, 512], BF16) for i in range(2)]
        FB = [psum("a_F%d" % i, [128, 512], F32) for i in range(6)]
        FK = ["psF%d" % i for i in range(6)]
        QB3 = [0, 1, 4]
        ps_s = [FB[0], FB[1], FB[4]]
        ps_sk = [FK[0], FK[1], FK[4]]
        DUMMY = int(os.environ.get("DUMMY", "0"))

        def pe_warm(nrep):
            for _ in range(nrep):
                mm(FB[5][:, 0:512], ident_b[:], amask[:], True, True, r=[], w=[FK[5]])
        ps_o = [FB[2], FB[3]]
        ps_ok = [FK[2], FK[3]]
        ps_b = FB[5]
        for g in range(3):
            memset(Vg[g][:, :, 64:65], 1.0, w=["V%d" % g])
        pcount = [0]
        ocount = [0]
        vcount = [0]
        NHL = 0 if SKIPA else (NHEADS if stop_after != "A1" else 1)

        rbn = [A("a_rbn%d" % i, [64, 512], F32) for i in range(2)]

        def norm_block(hh, cb):
            cs_ = slice(cb * 512, (cb + 1) * 512)
            i = cb % 2
            mm(ps_b[0:64, :], ones_f[64:65, 0:64], Oacc[64:65, cs_], True, True, r=["Oacc"], w=[FK[5]])
            recip(rbn[i][:], ps_b[0:64, :], r=[FK[5]], w=["rbn%d" % i])
            tt(yA[:, cs_], Oacc[0:64, cs_], rbn[i][:], ALU.mult, r=["rbn%d" % i, "Oacc"], w=["yA"], eng="gpsimd")
            if cb == L // 512 - 1:
                dma(A_d[sq, hh * 64:(hh + 1) * 64, :], yA[:], r=["yA"], q="gpsimd")

        for h in range(NHL):
            for sl in range(6):
                c0 = (2048 if sl < 3 else 3584) + (sl % 3) * 512 + h * 64
                dma(Wqk[:, :, sl * 64:(sl + 1) * 64],
                    Wb_in[:, c0:c0 + 64].rearrange("(dt p) c -> p dt c", p=128), w=["Wqk"])
            for g in range(3):
                c0 = 5120 + g * 512 + h * 64
                dma(Wv[:, :, g * 64:(g + 1) * 64],
                    Wb_in[:, c0:c0 + 64].rearrange("(dt p) c -> p dt c", p=128), w=["Wv"])
            vjobs = []
            for g in range(3):
                for kb0 in range(0, NB, 8):
                    vjobs.append((g, kb0))

            def do_vjob(g, kb0):
                dil = DILS[g]
                nb = L // (128 * dil)
                vb = vcount[0] % 2
                vcount[0] += 1
                pv_ = FB[2 + vb][:].rearrange("p (a b) -> p a b", a=8)
                for j in range(8):
                    kb = kb0 + j
                    r_, i_ = kb // nb, kb % nb
                    o_ = r_ + dil * 128 * i_
                    for dt_ in range(DT):
                        mm(pv_[:, j, :], hT[:, dt_, o_:o_ + dil * 127 + 1:dil], Wv[:, dt_, g * 64:(g + 1) * 64],
                           dt_ == 0, dt_ == DT - 1, r=["Wv"], w=[FK[2 + vb]])
                cp(Vg[g][:, kb0:kb0 + 8, 0:64], pv_, r=[FK[2 + vb]], w=["V%d" % g], eng="vector")

            vdone = 0

            def qk_mm(t):
                fb = QB3[t % 3]
                for dt_ in range(DT):
                    mm(FB[fb][:, 0:384], hT[:, dt_, t * 128:(t + 1) * 128],
                       Wqk[:, dt_, :], dt_ == 0, dt_ == DT - 1, r=["Wqk"], w=[FK[fb]])

            def qk_s1(t):
                b = t % 3
                pq3 = FB[QB3[b]][:, 0:384].rearrange("p (a b) -> p a b", a=6)
                kq = FK[QB3[b]]
                B_ = str(b)
                act(sq_s[b][:], pq3, AF.Square, r=[kq], w=["sq_s" + B_])
                P.op("vector", lambda e, b=b: e.tensor_reduce(ss6[b][:], sq_s[b][:], AX.X, ALU.add), r=["sq_s" + B_],
                     w=["ss6" + B_])
                act(rstd6[b][:], ss6[b][:], AF.Sqrt, r=["ss6" + B_], w=["rstd6" + B_], bias=EPS_AP[:], scale=1.0 / 64)
                recip(rstd6[b][:], rstd6[b][:], r=["rstd6" + B_], w=["rstd6" + B_])
                tt(qn[b][:], pq3, rstd6[b][:].unsqueeze(2).to_broadcast([128, 6, 64]), ALU.mult,
                   r=[kq, "rstd6" + B_], w=["qn" + B_])
                tt(qn[b][:], qn[b][:], qkw_rep[:], ALU.mult, r=["qn" + B_], w=["qn" + B_], eng="gpsimd")

            def qk_s2(t):
                b = t % 3
                B_ = str(b)
                c2_ = cos2[:, t, :].unsqueeze(1).to_broadcast([128, 6, 16])
                s2_ = sin2[:, t, :].unsqueeze(1).to_broadcast([128, 6, 16])
                xs = qn[b][:, :, 0:16]
                xr = sap(qn[b], 0, 128, 8, [[64, 6], [-8, 2], [1, 8]])
                r0, r1 = rr4[b][0], rr4[b][1]
                tt(r0[:], xs, c2_, ALU.mult, r=["qn" + B_], w=["r0" + B_])
                tt(r1[:].rearrange("p a (h e) -> p a h e", h=2), xr,
                   sin2[:, t, :].rearrange("p (h e) -> p h e", h=2).unsqueeze(1).to_broadcast([128, 6, 2, 8]),
                   ALU.mult, r=["qn" + B_], w=["r1" + B_])
                cp(qb[b][:, :, 16:64], qn[b][:, :, 16:64], r=["qn" + B_], w=["qb2" + B_], eng="scalar")
                tt(qb[b][:, :, 0:16], r0[:], r1[:], ALU.add, r=["r0" + B_, "r1" + B_], w=["qb0" + B_])

            def qk_tp(t):
                b = t % 3
                tb = t % 2
                B_ = str(b)
                qbf = qb[b][:].rearrange("p a b -> p (a b)")
                QB = ["qb0" + B_, "qb2" + B_]
                kT = "psT" + str(tb)
                tp(ps_T[tb][:, 0:128], qbf[:, 0:128], ident_b[:], r=QB, w=[kT])
                tp(ps_T[tb][:, 128:256], qbf[:, 192:320], ident_b[:], r=QB, w=[kT])
                tp(ps_T[tb][0:64, 256:384], qbf[:, 128:192], ident_b[:], r=QB, w=[kT])
                tp(ps_T[tb][0:64, 384:512], qbf[:, 320:384], ident_b[:], r=QB, w=[kT])
                cp(QK01[:, :, t * 128:(t + 1) * 128], ps_T[tb][:, 0:256].rearrange("p (a b) -> p a b", a=2),
                   r=[kT], w=["QK"], eng="scalar")
                cp(QK2[:, :, t * 128:(t + 1) * 128], ps_T[tb][0:64, 256:512].rearrange("p (a b) -> p a b", a=2),
                   r=[kT], w=["QK"], eng="scalar")

            qk_mm(0)
            qk_mm(1)
            for i in range(NT + 2):
                if i + 2 < NT:
                    qk_mm(i + 2)
                if h > 0 and 2 <= i < 2 + L // 512:
                    norm_block(h - 1, i - 2)
                if i < NT:
                    qk_s1(i)
                if 0 <= i - 1 < NT:
                    qk_s2(i - 1)
                if DUMMY and not (h > 0 and 1 <= i < 3 + L // 512):
                    pe_warm(DUMMY)
                want = (min(i + 1, NT) * len(vjobs)) // NT
                while vdone < want:
                    do_vjob(*vjobs[vdone])
                    vdone += 1
                if 0 <= i - 2 < NT:
                    qk_tp(i - 2)
            pairs = []
            for g in range(3):
                dil = DILS[g]
                nb = L // (128 * dil)
                for r_ in range(dil):
                    for pp in range((nb + 1) // 2):
                        pairs.append((g, r_, pp))

            def qkviews(g):
                if g < 2:
                    return QK01[g * 64:(g + 1) * 64, 0, :], QK01[g * 64:(g + 1) * 64, 1, :]
                return QK2[0:64, 0, :], QK2[0:64, 1, :]

            def pair_geom(g, r_, pp):
                dil = DILS[g]
                nb = L // (128 * dil)
                i0, i1 = 2 * pp, 2 * pp + 1
                a = 256 if i0 < nb - 1 else 128
                b = 0 if i1 >= nb else (256 if i1 < nb - 1 else 128)
                return dil, nb, i0, i1, a, b

            def emit_S(n):
                g, r_, pp = pairs[n]
                dil, nb, i0, i1, a, b = pair_geom(g, r_, pp)
                Qv, Kv = qkviews(g)
                sb = n % 3
                o0 = r_ + dil * 128 * i0
                mm(ps_s[sb][:, 0:a], Kv[:, o0:o0 + dil * 127 + 1:dil], Qv[:, o0:o0 + dil * (a - 1) + 1:dil], True, True,
                   r=["QK"], w=[ps_sk[sb]])
                if b:
                    o1 = r_ + dil * 128 * i1
                    mm(ps_s[sb][:, 256:256 + b], Kv[:, o1:o1 + dil * 127 + 1:dil],
                       Qv[:, o1:o1 + dil * (b - 1) + 1:dil], True, True, r=["QK"], w=[ps_sk[sb]])

            def emit_E(n):
                g, r_, pp = pairs[n]
                dil, nb, i0, i1, a, b = pair_geom(g, r_, pp)
                sb = n % 3
                pm = n % 4
                nn = 256 + b if b else a
                act(Pm[pm][:, 0:nn], ps_s[sb][:, 0:nn], AF.Exp, r=[ps_sk[sb]], w=["Pm%d" % pm], scale=0.125)
                tt(Pm[pm][:, 0:nn], Pm[pm][:, 0:nn], amask[:, 0:nn], ALU.mult, r=["Pm%d" % pm], w=["Pm%d" % pm],
                   eng="gpsimd" if n % 3 else "vector")

            ob = None
            NP_ = len(pairs)
            emit_S(0)
            if NP_ > 1:
                emit_S(1)
            emit_E(0)
            for n in range(NP_):
                g, r_, pp = pairs[n]
                dil, nb, i0, i1, a, b = pair_geom(g, r_, pp)
                if n + 2 < NP_:
                    emit_S(n + 2)
                if n + 1 < NP_:
                    emit_E(n + 1)
                pm = n % 4
                prevPm = (n - 1) % 4
                for (i_, curc, prevsrc) in ((i0, 0, (prevPm, 384)), (i1, 256, (pm, 128))):
                    if i_ >= nb:
                        continue
                    slot = i_ % 4
                    if slot == 0:
                        ob = ocount[0] % 2
                        ocount[0] += 1
                    kbi = r_ * nb + i_
                    outp = ps_o[ob][0:65, slot * 128:(slot + 1) * 128]
                    if i_ > 0:
                        ppm, pc = prevsrc
                        mm(outp, Vg[g][:, kbi - 1, :], Pm[ppm][:, pc:pc + 128], True, False,
                           r=["V%d" % g, "Pm%d" % ppm], w=[ps_ok[ob]])
                        mm(outp, Vg[g][:, kbi, :], Pm[pm][:, curc:curc + 128], False, True,
                           r=["V%d" % g, "Pm%d" % pm], w=[ps_ok[ob]])
                    else:
                        mm(outp, Vg[g][:, kbi, :], Pm[pm][:, curc:curc + 128], True, True,
                           r=["V%d" % g, "Pm%d" % pm], w=[ps_ok[ob]])
                    if slot == 3 or i_ == nb - 1:
                        nq = slot + 1
                        ib = i_ - slot
                        ov = sap(Oacc, 0, 65, r_ + dil * 128 * ib, [[dil * 128, nq], [dil, 128]])
                        pv = ps_o[ob][0:65, 0:nq * 128].rearrange("p (a b) -> p a b", a=nq)
                        if g == 0:
                            cp(ov, pv, r=[ps_ok[ob]], w=["Oacc"], eng="scalar")
                        else:
                            tt(ov, pv, ov, ALU.add, r=[ps_ok[ob]], w=["Oacc"])
            if h == NHL - 1:
                for cb in range(L // 512):
                    norm_block(h, cb)
        P.barrier()
        es_phase[0].close()
        if stop_after in ("A", "A1"):
            if "dbg_A" in dbg_d:
                dma(dbg_d["dbg_A"], A_d[sq])
            return finish(nc, P, es, out_d, None)

        es_phase[0] = ExitStack()
        RB = [A("s_R%d" % i, [128, 8192], BF16) for i in range(3)]
        RK = ["R0", "R1", "R2"]
        X8 = A("s_X8", [128, 129, 2, 32], F32)
        sc1 = A("s_sc1", [128, 2, 32], F32)
        sc2 = A("s_sc2", [128, 2, 32], F32)
        Hb = [A("s_Hb%d" % i, [128, 2, 128], BF16) for i in range(2)]
        SWb = [A("s_SWb%d" % i, [128, 4, 128], BF16) for i in range(4)]
        SWc = [A("s_SWc%d" % i, [128, 6, 128], BF16) for i in range(4)]
        WS = [A("s_WS%d" % i, [128, 8, 256], BF16) for i in range(2)]
        gsq_ = [A("s_gsq%d" % i, [128, 512], F32) for i in range(2)]
        gt2_ = [A("s_gt2%d" % i, [128, 512], F32) for i in range(2)]
        gsg_ = [A("s_gsg%d" % i, [128, 512], F32) for i in range(2)]
        sgb = [A("s_sgb%d" % i, [128, 512], BF16) for i in range(2)]
        gate4 = [A("s_gate%d" % i, [128, 512], BF16) for i in range(4)]
        srf_ = gsq_
        Abt = A("s_Abt", [128, 4, 1024], BF16)
        xt_ = [A("s_xt%d" % i, [128, 256], F32) for i in range(4)]
        ot_ = [A("s_ot%d" % i, [128, 256], F32) for i in range(4)]
        ps_a = [psum("s_ps_a%d" % i, [128, 512], F32) for i in range(2)]
        ps_t = [psum("s_ps_t%d" % i, [128, 1024], BF16) for i in range(4)]
        ps_y = [psum("s_ps_y%d" % i, [128, 4, 128], F32) for i in range(2)]
        wsc = [0]
        cnt = [0]
        tbc = [0]
        tcnt = [0]
        TB = [ps_a[0], ps_a[1],
              ps_y[0][:].rearrange("p a b -> p (a b)"), ps_y[1][:].rearrange("p a b -> p (a b)")]
        TK = ["ps_a0", "ps_a1", "ps_y0", "ps_y1"]
        NTB = 4

        def v_g(i):
            return RB[i][:].rearrange("p (g c) -> p g c", g=64)

        def v_t(i):
            return RB[i][:].rearrange("p (a t) -> p a t", a=8)

        def bmap(b):
            return ((1 + b) % 3, (2 + b) % 3, (0 + b) % 3)

        def load_ws(src, ktiles, c0):
            b = wsc[0] % 2
            wsc[0] += 1
            dma(WS[b][:, 0:ktiles, :], src[:, c0:c0 + 256].rearrange("(kt p) c -> p kt c", p=128), w=["WS%d" % b])
            return b

        def alt():
            cnt[0] += 1
            return "scalar" if cnt[0] % 2 else "vector"

        def tmpb():
            tcnt[0] += 1
            return tcnt[0] % 2

        def fm_proj(ktiles, rhs_fn, rkeys, c2, wb):
            k = tbc[0] % NTB
            tbc[0] += 1
            for kt in range(ktiles):
                mm(TB[k][:, 0:512], WS[wb][:, kt, c2 * 128:(c2 + 1) * 128], rhs_fn(kt), kt == 0, kt == ktiles - 1,
                   r=["WS%d" % wb] + rkeys, w=[TK[k]])
            return k

        def front(blk):
            X, Y, Z = bmap(blk)
            T0 = blk * 1024
            Ib, Ik = RB[Z], RK[Z]
            Uv, Uk = v_g(X), RK[X]
            if blk > 0:
                cp(X8[:, 0, :, :], X8[:, 128, :, :], r=["X8"], w=["X8"])
            for cq in range(4):
                wb = load_ws(Wb_in, 8, cq * 256)
                for s in range(8):
                    k = (cq * 8 + s) % 2
                    for dt_ in range(DT):
                        mm(ps_a[k][:, 0:256], hT[:, dt_, T0 + s:T0 + s + 8 * 127 + 1:8], WS[wb][:, dt_, :],
                           dt_ == 0, dt_ == DT - 1, r=["WS%d" % wb], w=["ps_a%d" % k])
                    io = sap(Ib, 0, 128, cq * 16 * 128 + s * 16, [[128, 16], [1, 16]])
                    cp(io, ps_a[k][:, 0:256].rearrange("p (a b) -> p a b", a=16), r=["ps_a%d" % k], w=[Ik], eng=alt())
            for g8 in range(8):
                k = g8 % 4
                for j in range(8):
                    g = g8 * 8 + j
                    tp(ps_t[k][:, j * 128:(j + 1) * 128], Ib[:, g * 128:(g + 1) * 128], ident_b[:], r=[Ik],
                       w=["ps_t%d" % k])
                cp(Uv[:, g8 * 8:(g8 + 1) * 8, :], ps_t[k][:].rearrange("p (a b) -> p a b", a=8), r=["ps_t%d" % k],
                   w=[Uk], eng=alt())
            for gp2 in range(16):
                pxb = ps_y[gp2 % 2][:].rearrange("p (a b) c -> p a b c", a=2)
                pxk = "ps_y%d" % (gp2 % 2)
                for j in range(2):
                    gp = gp2 * 2 + j
                    sb = gp % 4
                    dma(SWb[sb][:], SSMW_d[:, gp * NSL * 128:gp * NSL * 128 + 4 * 128].rearrange("p (a b) -> p a b", a=4),
                        w=["SWb%d" % sb])
                    for ri in range(2):
                        mm(pxb[:, j, ri, :], SWb[sb][:, 2 * ri, :], Uv[:, 2 * gp, :], True, False,
                           r=["SWb%d" % sb, Uk], w=[pxk])
                        mm(pxb[:, j, ri, :], SWb[sb][:, 2 * ri + 1, :], Uv[:, 2 * gp + 1, :], False, True,
                           r=["SWb%d" % sb, Uk], w=[pxk])
                xo = sap(X8, 0, 128, 64 + gp2 * 2, [[1, 2], [32, 2], [64, 128]])
                cp(xo, pxb[:], r=[pxk], w=["X8"], eng=alt())

        def scan_step(c):
            cur = X8[:, c, :, :]
            swp = sap(X8, 0, 128, c * 64 + 32, [[-32, 2], [1, 32]])
            nxt = X8[:, c + 1, :, :]
            tt(sc1[:], cur, a8[:, 0, :, :], ALU.mult, r=["X8"], w=["sc1"])
            tt(sc2[:], swp, a8[:, 1, :, :], ALU.mult, r=["X8"], w=["sc2"])
            tt(sc1[:], sc1[:], sc2[:], ALU.add, r=["sc1", "sc2"], w=["sc1"])
            tt(nxt, nxt, sc1[:], ALU.add, r=["sc1", "X8"], w=["X8"])

        def cstep(blk):
            X, Y, Z = bmap(blk)
            Uv, Uk = v_g(X), RK[X]
            YTb, YTk = RB[Y], RK[Y]
            YTv = v_g(Y)
            YIb, YIk = RB[Z], RK[Z]

            CB = [ps_y[0][:].rearrange("p a b -> p (a b)"), ps_y[1][:].rearrange("p a b -> p (a b)"),
                  ps_a[0][:, 0:512], ps_a[1][:, 0:512]]
            CK = ["ps_y0", "ps_y1", "ps_a0", "ps_a1"]

            def cstep_mm(gp2):
                k = gp2 % 4
                pyv = CB[k].rearrange("p (a b) -> p a b", a=4)
                for j in range(2):
                    gp = gp2 * 2 + j
                    sb = gp % 4
                    dma(SWc[sb][:], SSMW_d[:, gp * NSL * 128 + 4 * 128:gp * NSL * 128 + 10 * 128].rearrange(
                        "p (a b) -> p a b", a=6), w=["SWc%d" % sb])
                    hin = sap(X8, 0, 128, gp, [[32, 2], [64, 128]])
                    hb = gp % 2
                    cp(Hb[hb][:], hin, r=["X8"], w=["Hb%d" % hb], eng="gpsimd")
                    for g2 in range(2):
                        g = 2 * gp + g2
                        o_ = pyv[:, 2 * j + g2, :]
                        mm(o_, SWc[sb][:, 4 + g2, :], Uv[:, g, :], True, False, r=["SWc%d" % sb, Uk], w=[CK[k]])
                        mm(o_, SWc[sb][:, 0 + g2, :], Hb[hb][:, 0, :], False, False, r=["SWc%d" % sb, "Hb%d" % hb],
                           w=[CK[k]])
                        mm(o_, SWc[sb][:, 2 + g2, :], Hb[hb][:, 1, :], False, True, r=["SWc%d" % sb, "Hb%d" % hb],
                           w=[CK[k]])

            def gelu_a(gp2):
                k = gp2 % 2
                py = CB[gp2 % 4]
                pk = CK[gp2 % 4]
                K_ = str(k)
                act(gsq_[k][:], py, AF.Square, r=[pk], w=["gsq" + K_])
                ts(gsq_[k][:], gsq_[k][:], 0.044715, 1.0, ALU.mult, ALU.add, r=["gsq" + K_], w=["gsq" + K_])
                tt(gt2_[k][:], gsq_[k][:], py, ALU.mult, r=["gsq" + K_, pk], w=["gt2" + K_])

            def gelu_b(gp2):
                k = gp2 % 2
                py = CB[gp2 % 4]
                pk = CK[gp2 % 4]
                K_ = str(k)
                act(gsg_[k][:], gt2_[k][:], AF.Sigmoid, r=["gt2" + K_], w=["gsg" + K_], scale=1.5957691216057308)
                tt(YTb[:, gp2 * 512:(gp2 + 1) * 512], gsg_[k][:], py, ALU.mult, r=["gsg" + K_, pk], w=[YTk])

            for g0 in range(3):
                cstep_mm(g0)
            for gp2 in range(16):
                if gp2 > 0:
                    gelu_b(gp2 - 1)
                if gp2 + 3 < 16:
                    cstep_mm(gp2 + 3)
                gelu_a(gp2)
            gelu_b(15)
            for g8 in range(8):
                k = g8 % 4
                for j in range(8):
                    g = g8 * 8 + j
                    tp(ps_t[k][:, j * 128:(j + 1) * 128], YTv[:, g, :], ident_b[:], r=[YTk], w=["ps_t%d" % k])
                yo = sap(YIb, 0, 128, 16 * g8 * 8, [[16, 8], [1024, 8], [1, 16]])
                cp(yo, ps_t[k][:].rearrange("p (a b c) -> p a b c", a=8, b=8), r=["ps_t%d" % k], w=[YIk], eng=alt())
            YIv = v_t(Z)
            for ct in range(8):
                k = ct % 4
                for t in range(8):
                    tp(ps_t[k][:, t * 128:(t + 1) * 128], YIv[:, t, ct * 128:(ct + 1) * 128], ident_b[:], r=[YIk],
                       w=["ps_t%d" % k])
                go = sap(RB[X], 0, 128, ct * 1024, [[1, 8], [8, 128]])
                cp(go, ps_t[k][:].rearrange("p (a b) -> p a b", a=8), r=["ps_t%d" % k], w=[RK[X]], eng=alt())

        def tail(blk, mid_hook, fill_hook):
            X, Y, Z = bmap(blk)
            gTv, gK = v_t(X), RK[X]
            S1v, sK = v_t(Y), RK[Y]
            mv, mK = v_t(Z), RK[Z]
            T0 = blk * 1024
            gblk = slice(T0, T0 + 1024)
            dma(Abt[:], A_d[sq, :, gblk].rearrange("(kt p) t -> p kt t", p=128), w=["Abt"])
            halves = [(slice(0, 512), slice(T0, T0 + 512)), (slice(512, 1024), slice(T0 + 512, T0 + 1024))]
            for cq in range(4):
                wb = load_ws(Wb_glu, 8, cq * 256)
                for c2 in range(2):
                    co = 2 * cq + c2
                    for (tk, gtk) in halves:
                        k = fm_proj(8, lambda kt: gTv[:, kt, tk], [gK], c2, wb)
                        i = tmpb()
                        act(sgb[i][:], TB[k][:, 0:512], AF.Sigmoid, r=[TK[k]], w=["sgb%d" % i])
                        tt(S1v[:, co, tk], gTv[:, co, tk], sgb[i][:], ALU.mult, r=[gK, "sgb%d" % i], w=[sK])
            for cq in range(4):
                wb = load_ws(Wb_in, 8, 1024 + cq * 256)
                for c2 in range(2):
                    co = 2 * cq + c2
                    for (tk, gtk) in halves:
                        k = fm_proj(8, lambda kt: hT[:, kt, gtk], [], c2, wb)
                        i = tmpb()
                        act(srf_[i][:], TB[k][:, 0:512], AF.Sigmoid, r=[TK[k]], w=["gsq%d" % i])
                        tt(gt2_[i][:], TB[k][:, 0:512], srf_[i][:], ALU.mult, r=[TK[k], "gsq%d" % i], w=["gt2%d" % i])
                        tt(S1v[:, co, tk], S1v[:, co, tk], gt2_[i][:], ALU.mult, r=["gt2%d" % i], w=[sK], eng="gpsimd")
            for cq in range(4):
                wb2 = load_ws(Wb_in, 8, 7168 + cq * 256)
                gi = 0
                for c2 in range(2):
                    for (tk, gtk) in halves:
                        k2 = fm_proj(8, lambda kt: hT[:, kt, gtk], [], c2, wb2)
                        act(gate4[gi][:], TB[k2][:, 0:512], AF.Sigmoid, r=[TK[k2]], w=["gate%d" % gi])
                        gi += 1
                wb = load_ws(Wb_so, 8, cq * 256)
                gi = 0
                for c2 in range(2):
                    co = 2 * cq + c2
                    for (tk, gtk) in halves:
                        k = fm_proj(8, lambda kt: S1v[:, kt, tk], [sK], c2, wb)
                        tt(mv[:, co, tk], TB[k][:, 0:512], gate4[gi][:], ALU.mult, r=[TK[k], "gate%d" % gi], w=[mK])
                        gi += 1
            mid_hook()
            for cq in range(2):
                wb = load_ws(Wb_in, 8, 6656 + cq * 256)
                for c2 in range(2):
                    kt_ = 2 * cq + c2
                    for (tk, gtk) in halves:
                        k = fm_proj(8, lambda kt: hT[:, kt, gtk], [], c2, wb)
                        i = tmpb()
                        act(gt2_[i][:], TB[k][:, 0:512], AF.Silu, r=[TK[k]], w=["gt2%d" % i])
                        tt(Abt[:, kt_, tk], Abt[:, kt_, tk], gt2_[i][:], ALU.mult, r=["gt2%d" % i, "Abt"], w=["Abt"],
                           eng="gpsimd")
                        fill_hook()
            for cq in range(4):
                wb2 = load_ws(Wb_in, 8, 8192 + cq * 256)
                gi = 0
                for c2 in range(2):
                    for (tk, gtk) in halves:
                        k2 = fm_proj(8, lambda kt: hT[:, kt, gtk], [], c2, wb2)
                        act(gate4[gi][:], TB[k2][:, 0:512], AF.Sigmoid, r=[TK[k2]], w=["gate%d" % gi])
                        gi += 1
                        fill_hook()
                wb = load_ws(Wb_ao, 4, cq * 256)
                gi = 0
                for c2 in range(2):
                    co = 2 * cq + c2
                    for (tk, gtk) in halves:
                        k = fm_proj(4, lambda kt: Abt[:, kt, tk], ["Abt"], c2, wb)
                        i = tmpb()
                        cp(gt2_[i][:], TB[k][:, 0:512], r=[TK[k]], w=["gt2%d" % i], eng="scalar")
                        tt(gt2_[i][:], gt2_[i][:], gate4[gi][:], ALU.mult, r=["gt2%d" % i, "gate%d" % gi],
                           w=["gt2%d" % i], eng="gpsimd")
                        tt(mv[:, co, tk], gt2_[i][:], mv[:, co, tk], ALU.add, r=["gt2%d" % i], w=[mK], eng="gpsimd")
                        gi += 1
                        fill_hook()
            for cq in range(4):
                wb = load_ws(Wb_o, 8, cq * 256)
                for t8 in range(8):
                    k = tbc[0] % NTB
                    tbc[0] += 1
                    xb = t8 % 4
                    tok0 = T0 + t8 * 128
                    dma(xt_[xb][:], x_d[sq, tok0:tok0 + 128, cq * 256:(cq + 1) * 256], w=["xt%d" % xb])
                    for kt in range(8):
                        mm(TB[k][:, 0:256], mv[:, kt, t8 * 128:(t8 + 1) * 128],
                           WS[wb][:, kt, :], kt == 0, kt == 7, r=["WS%d" % wb, mK], w=[TK[k]])
                    tt(ot_[xb][:], TB[k][:, 0:256], xt_[xb][:], ALU.add, r=[TK[k], "xt%d" % xb],
                       w=["ot%d" % xb])
                    dma(out_d[sq, tok0:tok0 + 128, cq * 256:(cq + 1) * 256], ot_[xb][:], r=["ot%d" % xb], q="gpsimd")
                    fill_hook()

        memset(X8[:, 0, :, :], 0.0, w=["X8"])
        front(0)
        for c in range(128):
            scan_step(c)
        for blk in range(NBLK):
            cstep(blk)
            pending = []

            def mid_hook(blk=blk, pending=pending):
                if blk + 1 < NBLK:
                    front(blk + 1)
                    pending.extend(range(128))

            def fill_hook(pending=pending):
                for _ in range(2):
                    if pending:
                        scan_step(pending.pop(0))

            tail(blk, mid_hook, fill_hook)
            while pending:
                scan_step(pending.pop(0))
        P.barrier()
        es_phase[0].close()
    return finish(nc, P, es, out_d, None)


def finish(nc, P, es, out_d, ph):
    P.barrier()
    P.emit()
    if ph is not None:
        ph.close()
    es.close()
    return nc


def host_consts():
    c = {}
    c["ident_bf"] = np.eye(128, dtype=np.float32).astype(ml_dtypes.bfloat16)
    c["ident_f"] = np.eye(128, dtype=np.float32)
    j = np.arange(128)[:, None]
    i = np.arange(128)[None, :]
    cur = (j <= i).astype(np.float32)
    prev = (j >= i).astype(np.float32)
    c["amask"] = np.concatenate([cur, prev, cur, prev], axis=1).astype(ml_dtypes.bfloat16)
    s_ = (np.arange(128) // 16)[:, None]
    t_ = (np.arange(128) // 16)[None, :]
    c["trit"] = (t_ >= s_).astype(np.float32)
    c["evals"] = np.tile(np.asarray(EVALS, np.float32)[None, :], (128, 1))
    invf = (500000.0 ** (-np.arange(0, 16, 2, dtype=np.float32) / 16)).astype(np.float32)
    c["invf"] = np.tile(invf[None, :], (128, 1))
    return c


def core_inputs(inputs, seqs, L):
    m = {}
    m["x"] = np.ascontiguousarray(inputs["x"][seqs, :L, :])
    m["pos"] = np.ascontiguousarray(inputs["positions"][seqs, :L]).reshape(len(seqs), L // 128, 128).astype(np.int32)
    m["norm_w"] = np.ascontiguousarray(inputs["norm_w"]).reshape(1, D)
    m["w_in"] = np.ascontiguousarray(inputs["w_in"]).reshape(D, 9216)
    m["lam_re"] = np.ascontiguousarray(inputs["lam_re"]).reshape(32, 128)
    m["lam_im"] = np.ascontiguousarray(inputs["lam_im"]).reshape(32, 128)
    m["log_dt"] = np.ascontiguousarray(inputs["log_dt"]).reshape(32, 2)
    m["b_re"] = np.ascontiguousarray(inputs["b_re"]).reshape(-1)
    m["b_im"] = np.ascontiguousarray(inputs["b_im"]).reshape(-1)
    m["c_re"] = np.ascontiguousarray(inputs["c_re"]).reshape(64, 16, 64)
    m["c_im"] = np.ascontiguousarray(inputs["c_im"]).reshape(64, 16, 64)
    m["d_skip"] = np.ascontiguousarray(inputs["d_skip"]).reshape(64, 16)
    m["w_glu"] = np.ascontiguousarray(inputs["w_glu"]).reshape(D, D)
    m["q_norm_w"] = np.ascontiguousarray(inputs["q_norm_w"]).reshape(1, 192)
    m["k_norm_w"] = np.ascontiguousarray(inputs["k_norm_w"]).reshape(1, 192)
    m["w_ssm_out"] = np.ascontiguousarray(inputs["w_ssm_out"]).reshape(D, D)
    m["w_attn_out"] = np.ascontiguousarray(inputs["w_attn_out"]).reshape(512, D)
    m["w_o"] = np.ascontiguousarray(inputs["w_o"]).reshape(D, D)
    m.update(host_consts())
    return {k: np.asarray(v) for k, v in m.items()}


_NC_CACHE = {}


def kernel(**inputs):
    B, L, _ = inputs["x"].shape
    nseq = B // NCORES
    key = (L, nseq)
    if key not in _NC_CACHE:
        _NC_CACHE[key] = build(L, nseq)
    nc = _NC_CACHE[key]
    in_maps = [core_inputs(inputs, list(range(c * nseq, (c + 1) * nseq)), L) for c in range(NCORES)]
    res = run_bass_kernel_spmd(nc, in_maps, core_ids=list(range(NCORES)))
    out = np.concatenate([np.asarray(r["out"]) for r in res.results], axis=0)
    return out.astype(np.float32)
```

```python
import math
from contextlib import ExitStack

import ml_dtypes
import numpy as np

import concourse.bass as bass
import concourse.mybir as mybir
from concourse.bass_utils import run_bass_kernel_spmd

F32 = mybir.dt.float32
BF16 = mybir.dt.bfloat16
I32 = mybir.dt.int32
AF = mybir.ActivationFunctionType
ALU = mybir.AluOpType
AX = mybir.AxisListType

D = 1024
DT = 8
NCORES = 8
EPS = 1e-6
ENGS = ["tensor", "vector", "scalar", "gpsimd", "sync"]
PI = math.pi

EVALS = [float(7 - i) for i in range(16)] + [float(i) for i in range(1, 9)]
NE = len(EVALS)


import os as _os
STRICT = bool(int(_os.environ.get("STRICT_SYNC", "1")))


class Prog:
    def __init__(self, nc, es):
        self.nc = nc
        self.ops = []
        self.stream = {e: [] for e in ENGS}
        self.res_w = {}
        self.res_r = {}
        self.last = {e: None for e in ENGS}
        self.sems = {e: es.enter_context(nc.semaphore(f"s_{e}")) for e in ENGS[:4]}
        self.qpool = {"sync": list(range(0, 16)), "gpsimd": list(range(16, 24)), "scalar": list(range(24, 32))}
        self.dsems = [es.enter_context(nc.semaphore(f"dq{i}")) for i in range(32)]
        self.qn = {"sync": 0, "gpsimd": 0, "scalar": 0}
        self.dma_prev = {}
        self.dma_out = []

    def op(self, eng, fn, r=(), w=(), dma=False, extra=()):
        oid = len(self.ops)
        deps = set(extra)
        psr = [k for k in r if k.startswith("ps")]
        w = list(w) + psr
        r = [k for k in r if not k.startswith("ps")]
        for k in r:
            x = self.res_w.get(k)
            if x is not None:
                deps.add(x)
        for k in w:
            x = self.res_w.get(k)
            if x is not None:
                ox = self.ops[x]
                if STRICT or dma or ox["dma"] or ox["eng"] != eng or k in psr:
                    deps.add(x)
            rr = self.res_r.get(k)
            if rr:
                for ek, rid in rr.items():
                    if STRICT or dma or ek != eng:
                        deps.add(rid)
        o = dict(id=oid, eng=eng, fn=fn, deps=deps, dma=dma, sig=False)
        if dma:
            pool = self.qpool[eng]
            idx = self.qn[eng]
            self.qn[eng] += 1
            o["dsem"] = pool[idx % len(pool)]
            o["dval"] = 16 * (idx // len(pool) + 1)
            prev = self.dma_prev.get(o["dsem"])
            if prev is not None:
                deps.add(prev)
            self.dma_prev[o["dsem"]] = oid
            self.dma_out.append(oid)
        self.ops.append(o)
        self.stream[eng].append(oid)
        for k in r:
            d = self.res_r.setdefault(k, {})
            d[("dma", oid) if dma else eng] = oid
        for k in w:
            self.res_w[k] = oid
            self.res_r[k] = {}
        self.last[eng] = oid
        return oid

    def barrier(self):
        deps = set(x for x in self.last.values() if x is not None)
        deps.update(self.dma_out)
        self.dma_out = []
        for e in ENGS:
            self.op(e, None, extra=deps)
        self.res_w = {}
        self.res_r = {}

    def emit(self):
        ops = self.ops
        for o in ops:
            keep = set()
            for d in o["deps"]:
                od = ops[d]
                if od["fn"] is None:
                    continue
                if (not od["dma"]) and (not o["dma"]) and od["eng"] == o["eng"] == "tensor":
                    continue
                keep.add(d)
                if not od["dma"]:
                    od["sig"] = True
            o["deps"] = keep
        cnt = {e: 0 for e in ENGS}
        for o in ops:
            if o["sig"]:
                cnt[o["eng"]] += 1
                o["cnt"] = cnt[o["eng"]]
        nc = self.nc
        with nc.Block() as block:
            for e in ENGS:
                def body(eng, e=e):
                    waited = {}
                    for oid in self.stream[e]:
                        o = ops[oid]
                        need = {}
                        for d in o["deps"]:
                            od = ops[d]
                            if od["dma"]:
                                key = ("d", od["dsem"])
                                val = od["dval"]
                            else:
                                key = ("e", od["eng"])
                                val = od["cnt"]
                            if val > need.get(key, 0):
                                need[key] = val
                        for key, val in need.items():
                            if waited.get(key, 0) >= val:
                                continue
                            waited[key] = val
                            sem = self.dsems[key[1]] if key[0] == "d" else self.sems[key[1]]
                            eng.wait_ge(sem, val)
                        if o["fn"] is None:
                            continue
                        ins = o["fn"](eng)
                        if o["dma"]:
                            ins.then_inc(self.dsems[o["dsem"]], 16)
                        elif o["sig"]:
                            ins.then_inc(self.sems[e], 1)
                getattr(block, e)(body)


def sap(t, part0, nparts, off, dims):
    full = t[:]
    pstride = full.ap[0][0]
    return bass.AP(full.tensor, part0 * pstride + off, [[pstride, nparts]] + [list(d) for d in dims])


import os
CPENG = os.environ.get("CPENG", "vector,scalar").split(",")
SKIPA = bool(int(os.environ.get("SKIPA", "0")))
NHEADS = int(os.environ.get("NHEADS", "8"))


def build(L, NSEQ, stop_after=None, dbg=()):
    NT = L // 128
    NBLK = L // 1024
    nc = bass.Bass("TRN2", target_bir_lowering=False)
    dram = {}

    def din(name, shape, dt):
        dram[name] = nc.dram_tensor(name, list(shape), dt, kind="ExternalInput").ap()
        return dram[name]

    x_d = din("x", [NSEQ, L, D], F32)
    pos_d = din("pos", [NSEQ, NT, 128], I32)
    normw_d = din("norm_w", [1, D], F32)
    win_d = din("w_in", [D, 9216], F32)
    lamre_d = din("lam_re", [32, 128], F32)
    lamim_d = din("lam_im", [32, 128], F32)
    logdt_d = din("log_dt", [32, 2], F32)
    bre_d = din("b_re", [64 * 64 * 16], F32)
    bim_d = din("b_im", [64 * 64 * 16], F32)
    cre_d = din("c_re", [64, 16, 64], F32)
    cim_d = din("c_im", [64, 16, 64], F32)
    dskip_d = din("d_skip", [64, 16], F32)
    wglu_d = din("w_glu", [D, D], F32)
    qnw_d = din("q_norm_w", [1, 192], F32)
    knw_d = din("k_norm_w", [1, 192], F32)
    wso_d = din("w_ssm_out", [D, D], F32)
    wao_d = din("w_attn_out", [512, D], F32)
    wo_d = din("w_o", [D, D], F32)
    identb_d = din("ident_bf", [128, 128], BF16)
    identf_d = din("ident_f", [128, 128], F32)
    amask_d = din("amask", [128, 512], BF16)
    trit_d = din("trit", [128, 128], F32)
    evals_d = din("evals", [128, NE], F32)
    invf_d = din("invf", [128, 8], F32)
    out_d = nc.dram_tensor("out", [NSEQ, L, D], F32, kind="ExternalOutput").ap()
    dbg_d = {}
    for name, shape, dt in dbg:
        dbg_d[name] = nc.dram_tensor(name, list(shape), dt, kind="ExternalOutput").ap()

    Wb_in = nc.dram_tensor("Wb_in", [D, 9216], BF16).ap()
    Wb_glu = nc.dram_tensor("Wb_glu", [D, D], BF16).ap()
    Wb_so = nc.dram_tensor("Wb_so", [D, D], BF16).ap()
    Wb_ao = nc.dram_tensor("Wb_ao", [512, D], BF16).ap()
    Wb_o = nc.dram_tensor("Wb_o", [D, D], BF16).ap()
    NSL = 12
    SSMW_d = nc.dram_tensor("SSMW", [128, 32 * NSL * 128], BF16).ap()
    A_d = nc.dram_tensor("A_scr", [NSEQ, 512, L], BF16).ap()

    es = ExitStack()
    P = Prog(nc, es)
    uid = [0]

    def A(name, shape, dt):
        uid[0] += 1
        return es_phase[0].enter_context(nc.sbuf_tensor("%s_%d" % (name, uid[0]), list(shape), dt))

    def AG(name, shape, dt):
        return es.enter_context(nc.sbuf_tensor(name, list(shape), dt))

    def psum(name, shape, dt=F32):
        uid[0] += 1
        return es_phase[0].enter_context(nc.psum_tensor("%s_%d" % (name, uid[0]), list(shape), dt))

    es_phase = [es]

    def dma(out, in_, r=(), w=(), q="sync", **kw):
        return P.op(q, lambda e: e.dma_start(out=out, in_=in_, **kw), r=r, w=w, dma=True)

    def dmaP(out, in_, r=(), w=(), **kw):
        return dma(out, in_, r=r, w=w, q="scalar", **kw)

    def mm(out, lhsT, rhs, start, stop, r=(), w=()):
        return P.op("tensor", lambda e: e.matmul(out, lhsT, rhs, start=start, stop=stop), r=r, w=w)

    def tp(out, in_, ident, r=(), w=()):
        return P.op("tensor", lambda e: e.transpose(out, in_, ident), r=r, w=w)

    def act(out, in_, func, r=(), w=(), bias=0.0, scale=1.0, accum_out=None):
        if accum_out is None:
            return P.op("scalar", lambda e: e.activation(out, in_, func, bias=bias, scale=scale), r=r, w=w)
        return P.op("scalar", lambda e: e.activation(out, in_, func, bias=bias, scale=scale, accum_out=accum_out), r=r, w=w)

    def tt(out, in0, in1, op, r=(), w=(), eng="vector"):
        return P.op(eng, lambda e: e.tensor_tensor(out, in0, in1, op), r=r, w=w)

    def ts(out, in0, s1, s2, op0, op1=None, r=(), w=(), eng="vector"):
        if op1 is None:
            return P.op(eng, lambda e: e.tensor_scalar(out, in0, s1, None, op0), r=r, w=w)
        return P.op(eng, lambda e: e.tensor_scalar(out, in0, s1, s2, op0, op1), r=r, w=w)

    def stt(out, in0, scalar, in1, op0, op1, r=(), w=()):
        return P.op("vector", lambda e: e.scalar_tensor_tensor(out, in0, scalar, in1, op0, op1), r=r, w=w)

    def cp(out, in_, r=(), w=(), eng="vector"):
        if eng == "scalar":
            return P.op("scalar", lambda e: e.copy(out, in_), r=r, w=w)
        return P.op(eng, lambda e: e.tensor_copy(out, in_), r=r, w=w)

    def memset(ap, val, w=(), eng="vector"):
        return P.op(eng, lambda e: e.memset(ap, val), w=w)

    def recip(out, in_, r=(), w=()):
        return P.op("vector", lambda e: e.reciprocal(out, in_), r=r, w=w)

    def dbg_dump(name, src_ap, keys=()):
        if name in dbg_d:
            P.barrier()
            dma(dbg_d[name], src_ap, q="sync")
            P.barrier()

    def range_reduce_sincos(ang, sin_out, cos_out, shape, tmpname):
        n = int(np.prod(shape[1:]))
        ki = A(tmpname + "_ki", [128, n], I32)
        kf = A(tmpname + "_kf", [128, n], F32)
        r_ = A(tmpname + "_r", [128, n], F32)
        y = A(tmpname + "_y", [128, n], F32)
        t1 = A(tmpname + "_t1", [128, n], F32)
        K = tmpname
        angf = ang
        ts(ki[:], angf, 1.0 / (2 * PI), None, ALU.mult, r=[K + "ang"], w=[K + "ki"])
        cp(kf[:], ki[:], r=[K + "ki"], w=[K + "kf"])
        stt(r_[:], kf[:], -2 * PI, angf, ALU.mult, ALU.add, r=[K + "kf", K + "ang"], w=[K + "r"])
        for shift, dst in ((0.0, sin_out), (PI / 2, cos_out)):
            ts(y[:], r_[:], shift, None, ALU.add, r=[K + "r"], w=[K + "y"])
            ts(t1[:], y[:], PI, -2 * PI, ALU.is_gt, ALU.mult, r=[K + "y"], w=[K + "t1"])
            tt(y[:], y[:], t1[:], ALU.add, r=[K + "y", K + "t1"], w=[K + "y"])
            ts(t1[:], y[:], -PI, 2 * PI, ALU.is_lt, ALU.mult, r=[K + "y"], w=[K + "t1"])
            tt(y[:], y[:], t1[:], ALU.add, r=[K + "y", K + "t1"], w=[K + "y"])
            ts(y[:], y[:], PI, -PI, ALU.min, ALU.max, r=[K + "y"], w=[K + "y"])
            act(dst, y[:], AF.Sin, r=[K + "y"], w=[K + "out" + str(shift)])

    ident_b = AG("sb_ident_b", [128, 128], BF16)
    ident_f = AG("sb_ident_f", [128, 128], F32)
    amask = AG("sb_amask", [128, 512], BF16)
    qkw_rep = AG("qkw_rep", [128, 6, 64], F32)
    invf = AG("sb_invf", [128, 8], F32)
    a8 = AG("a8", [128, 2, 2, 32], F32)
    ones_f = AG("ones_f", [128, 64], F32)
    dma(ident_b[:], identb_d, w=["c0"])
    dma(ident_f[:], identf_d, w=["c1"])
    dma(amask[:], amask_d, w=["c2"])
    for g in range(3):
        dma(qkw_rep[:, g, :], qnw_d[:, g * 64:(g + 1) * 64].partition_broadcast(128), w=["c4%d" % g])
        dma(qkw_rep[:, 3 + g, :], knw_d[:, g * 64:(g + 1) * 64].partition_broadcast(128), w=["c5%d" % g])
    dma(invf[:], invf_d, w=["c6"])
    memset(ones_f[:], 1.0, w=["c7"])
    EPS_AP = AG("eps_ap", [128, 1], F32)
    memset(EPS_AP[:], EPS, w=["c8"])
    P.barrier()
    if stop_after == "C":
        return finish(nc, P, es, out_d, None)

    es_phaseP = ExitStack()
    es_phase[0] = es_phaseP
    if True:
        raw = A("p_raw", [32, 3, 128], F32)
        ldt = A("p_ldt", [32, 2], F32)
        dmaP(raw[:, 0, :], lamre_d, w=["raw0"])
        dmaP(raw[:, 1, :], lamim_d, w=["raw1"])
        dmaP(ldt[:], logdt_d, w=["ldt"])
        cp(raw[:, 2, :].rearrange("p (a b) -> p a b", a=2), ldt[:].unsqueeze(2).to_broadcast([32, 2, 64]),
           r=["ldt"], w=["raw2"])
        ps_t = psum("p_ps_t", [128, 3, 32], F32)
        for j in range(3):
            tp(ps_t[:, j, :], raw[:, j, :], ident_f[0:32, 0:32], r=["raw%d" % j], w=["ps_t"])
        lam = A("p_lam", [128, 3, 32], F32)
        cp(lam[:], ps_t[:], r=["ps_t"], w=["lam"])
        if stop_after == "P1":
            return finish(nc, P, es, out_d, es_phase[0])

        dtt = A("p_dt", [128, 32], F32)
        act(dtt[:], lam[:, 2, :], AF.Exp, r=["lam"], w=["dt"])
        lrdt = A("p_lrdt", [128, 32], F32)
        lidt = A("p_lidt", [128, 32], F32)
        tt(lrdt[:], lam[:, 0, :], dtt[:], ALU.mult, r=["lam", "dt"], w=["lrdt"])
        tt(lidt[:], lam[:, 1, :], dtt[:], ALU.mult, r=["lam", "dt"], w=["lidt"])
        ev = A("p_ev", [128, NE], F32)
        dmaP(ev[:], evals_d, w=["ev"])
        marg = A("p_marg", [128, 32, NE], F32)
        ang = A("p_ang", [128, 32, NE], F32)
        tt(marg[:], lrdt[:].unsqueeze(2).to_broadcast([128, 32, NE]),
           ev[:].unsqueeze(1).to_broadcast([128, 32, NE]), ALU.mult, r=["lrdt", "ev"], w=["marg"])
        tt(ang[:], lidt[:].unsqueeze(2).to_broadcast([128, 32, NE]),
           ev[:].unsqueeze(1).to_broadcast([128, 32, NE]), ALU.mult, r=["lidt", "ev"], w=["pPang"])
        mag = A("p_mag", [128, 32, NE], F32)
        act(mag[:], marg[:], AF.Exp, r=["marg"], w=["mag"])
        if stop_after == "P2":
            return finish(nc, P, es, out_d, es_phase[0])

        sn = A("p_sin", [128, 32 * NE], F32)
        cs = A("p_cos", [128, 32 * NE], F32)
        range_reduce_sincos(ang[:].rearrange("p a b -> p (a b)"), sn[:], cs[:], [128, 32 * NE], "pP")
        PRE = A("p_PRE", [128, 32, NE], F32)
        PIM = A("p_PIM", [128, 32, NE], F32)
        tt(PRE[:].rearrange("p a b -> p (a b)"), mag[:].rearrange("p a b -> p (a b)"), cs[:], ALU.mult,
           r=["mag", "pPout" + str(PI / 2)], w=["PRE"])
        tt(PIM[:].rearrange("p a b -> p (a b)"), mag[:].rearrange("p a b -> p (a b)"), sn[:], ALU.mult,
           r=["mag", "pPout0.0"], w=["PIM"])
        cp(a8[:, 0, 0, :], PRE[:, :, 23], r=["PRE"], w=["a8a"])
        cp(a8[:, 0, 1, :], PRE[:, :, 23], r=["PRE"], w=["a8b"])
        ts(a8[:, 1, 0, :], PIM[:, :, 23], -1.0, None, ALU.mult, r=["PIM"], w=["a8c"])
        cp(a8[:, 1, 1, :], PIM[:, :, 23], r=["PIM"], w=["a8d"])
        if stop_after == "P3":
            return finish(nc, P, es, out_d, es_phase[0])

        den = A("p_den", [128, 32], F32)
        t0 = A("p_t0", [128, 32], F32)
        t1_ = A("p_t1", [128, 32], F32)
        nr = A("p_nr", [128, 32], F32)
        fre = A("p_fre", [128, 32], F32)
        fim = A("p_fim", [128, 32], F32)
        LR = lam[:, 0, :]
        LI = lam[:, 1, :]
        tt(den[:], LR, LR, ALU.mult, r=["lam"], w=["den"])
        tt(t0[:], LI, LI, ALU.mult, r=["lam"], w=["t0"])
        tt(den[:], den[:], t0[:], ALU.add, r=["den", "t0"], w=["den"])
        recip(den[:], den[:], r=["den"], w=["den"])
        ts(nr[:], PRE[:, :, 16], -1.0, None, ALU.add, r=["PRE"], w=["nr"])
        tt(t0[:], nr[:], LR, ALU.mult, r=["nr", "lam"], w=["t0"])
        tt(t1_[:], PIM[:, :, 16], LI, ALU.mult, r=["PIM", "lam"], w=["t1"])
        tt(t0[:], t0[:], t1_[:], ALU.add, r=["t0", "t1"], w=["t0"])
        tt(fre[:], t0[:], den[:], ALU.mult, r=["t0", "den"], w=["fre"])
        tt(t0[:], PIM[:, :, 16], LR, ALU.mult, r=["PIM", "lam"], w=["t0"])
        tt(t1_[:], nr[:], LI, ALU.mult, r=["nr", "lam"], w=["t1"])
        tt(t0[:], t0[:], t1_[:], ALU.subtract, r=["t0", "t1"], w=["t0"])
        tt(fim[:], t0[:], den[:], ALU.mult, r=["t0", "den"], w=["fim"])
        Bre = A("p_Bre", [128, 32, 16], F32)
        Bim = A("p_Bim", [128, 32, 16], F32)
        for q4 in range(4):
            for (src, dstt, kk) in ((bre_d, Bre, "Bre"), (bim_d, Bim, "Bim")):
                s_ap = bass.AP(src.tensor, q4 * 8 * 2048, [[16, 128], [2048, 8], [1, 16]])
                dmaP(dstt[:, q4 * 8:(q4 + 1) * 8, :], s_ap, w=[kk + str(q4)])
        BBre = A("p_BBre", [128, 32, 16], F32)
        BBim = A("p_BBim", [128, 32, 16], F32)
        tb0 = A("p_tb0", [128, 32, 16], F32)
        Bk = ["Bre%d" % i for i in range(4)] + ["Bim%d" % i for i in range(4)]
        fre_b = fre[:].unsqueeze(2).to_broadcast([128, 32, 16])
        fim_b = fim[:].unsqueeze(2).to_broadcast([128, 32, 16])
        tt(BBre[:], Bre[:], fre_b, ALU.mult, r=Bk + ["fre"], w=["BBre"])
        tt(tb0[:], Bim[:], fim_b, ALU.mult, r=Bk + ["fim"], w=["tb0"])
        tt(BBre[:], BBre[:], tb0[:], ALU.subtract, r=["BBre", "tb0"], w=["BBre"])
        tt(BBim[:], Bim[:], fre_b, ALU.mult, r=Bk + ["fre"], w=["BBim"])
        tt(tb0[:], Bre[:], fim_b, ALU.mult, r=Bk + ["fim"], w=["tb0"])
        tt(BBim[:], BBim[:], tb0[:], ALU.add, r=["BBim", "tb0"], w=["BBim"])
        if stop_after == "P4":
            return finish(nc, P, es, out_d, es_phase[0])

        Cre = A("p_Cre", [128, 32, 16], F32)
        Cim = A("p_Cim", [128, 32, 16], F32)
        cst = A("p_cst", [128, 2, 2, 64], F32)
        ps_c = psum("p_ps_c", [128, 2, 128], F32)
        for k4 in range(4):
            for gq in range(8):
                gp = k4 * 8 + gq
                for ri, src in ((0, cre_d), (1, cim_d)):
                    s_ap = bass.AP(src.tensor, (2 * gp) * 1024, [[64, 16], [1024, 2], [1, 64]])
                    dmaP(cst[gq * 16:(gq + 1) * 16, ri, :, :], s_ap, w=["cst%d_%d" % (gq, ri)],
                        r=[])
            ck = ["cst%d_%d" % (gq, ri) for gq in range(8) for ri in range(2)]
            for ri in range(2):
                tp(ps_c[:, ri, :], cst[:, ri, :, :].rearrange("p a b -> p (a b)"), ident_f[:], r=ck, w=["ps_c"])
            cp(Cre[:, k4 * 8:(k4 + 1) * 8, :], ps_c[:, 0, :].rearrange("p (a b) -> p a b", a=8), r=["ps_c"], w=["Cre%d" % k4])
            cp(Cim[:, k4 * 8:(k4 + 1) * 8, :], ps_c[:, 1, :].rearrange("p (a b) -> p a b", a=8), r=["ps_c"], w=["Cim%d" % k4])
        Ck = ["Cre%d" % i for i in range(4)] + ["Cim%d" % i for i in range(4)]
        if stop_after == "P5":
            return finish(nc, P, es, out_d, es_phase[0])

        dsk0 = A("p_dsk0", [64, 16], F32)
        dsk1 = A("p_dsk1", [64, 8, 16], F32)
        dmaP(dsk0[:], dskip_d, w=["dsk0"])
        cp(dsk1[:], dsk0[:].unsqueeze(1).to_broadcast([64, 8, 16]), r=["dsk0"], w=["dsk1"])
        ps_d = psum("p_ps_d", [128, 64], F32)
        tp(ps_d[:], dsk1[:].rearrange("p a b -> p (a b)"), ident_f[0:64, 0:64], r=["dsk1"], w=["ps_d"])
        DSK = A("p_DSK", [128, 64], F32)
        cp(DSK[:], ps_d[:], r=["ps_d"], w=["DSK"])
        trit = A("p_trit", [128, 128], F32)
        dmaP(trit[:], trit_d, w=["trit"])
        if stop_after == "P6":
            return finish(nc, P, es, out_d, es_phase[0])


        SW = A("p_SW", [128, 8, NSL, 128], BF16)
        GR = A("p_GR", [128, 8, 16, 16], F32)
        GI = A("p_GI", [128, 8, 16, 16], F32)
        GT = A("p_GT", [128, 8, 16, 16], F32)
        GRb = A("p_GRb", [128, 8, 2, 128], BF16)
        WCf = A("p_WCf", [128, 8, 2, 2, 128], F32)
        WCt = A("p_WCt", [128, 8, 8, 16], F32)
        WCt2 = A("p_WCt2", [128, 8, 8, 16], F32)
        ps_w = psum("p_ps_w", [128, 2, 128], BF16)
        ps_z = psum("p_ps_z", [128, 2, 128], F32)
        tzt = A("p_tzt", [128, 2, 128], F32)
        memset(WCf[:], 0.0, w=["WCf"])
        for k4 in range(4):
            gsl = slice(k4 * 8, (k4 + 1) * 8)
            sh = [128, 8, 16, 16]
            pre_a = PRE[:, gsl, 0:16].unsqueeze(3).to_broadcast(sh)
            pim_a = PIM[:, gsl, 0:16].unsqueeze(3).to_broadcast(sh)
            bbr = BBre[:, gsl, :].unsqueeze(2).to_broadcast(sh)
            bbi = BBim[:, gsl, :].unsqueeze(2).to_broadcast(sh)
            tt(GR[:], pre_a, bbr, ALU.mult, r=["PRE", "BBre"], w=["GR"])
            tt(GT[:], pim_a, bbi, ALU.mult, r=["PIM", "BBim"], w=["GT"])
            tt(GR[:], GR[:], GT[:], ALU.subtract, r=["GR", "GT"], w=["GR"])
            tt(GI[:], pre_a, bbi, ALU.mult, r=["PRE", "BBim"], w=["GI"])
            tt(GT[:], pim_a, bbr, ALU.mult, r=["PIM", "BBre"], w=["GT"])
            tt(GI[:], GI[:], GT[:], ALU.add, r=["GI", "GT"], w=["GI"])
            cp(GRb[:, :, 0, :], GR[:, :, 0:8, :].rearrange("p g a b -> p g (a b)"), r=["GR"], w=["GRb0"])
            cp(GRb[:, :, 1, :], GI[:, :, 0:8, :].rearrange("p g a b -> p g (a b)"), r=["GI"], w=["GRb1"])
            if stop_after == "P7":
                return finish(nc, P, es, out_d, es_phase[0])

            sh2 = [128, 8, 8, 16]
            pre_c = PRE[:, gsl, 16:24].unsqueeze(3).to_broadcast(sh2)
            pim_c = PIM[:, gsl, 16:24].unsqueeze(3).to_broadcast(sh2)
            crb = Cre[:, gsl, :].unsqueeze(2).to_broadcast(sh2)
            cib = Cim[:, gsl, :].unsqueeze(2).to_broadcast(sh2)
            tt(WCt[:], crb, pre_c, ALU.mult, r=Ck + ["PRE"], w=["WCt"])
            tt(WCt2[:], cib, pim_c, ALU.mult, r=Ck + ["PIM"], w=["WCt2"])
            tt(WCt[:], WCt[:], WCt2[:], ALU.subtract, r=["WCt", "WCt2"], w=["WCt"])
            for hh in range(2):
                rows = slice(hh * 64, (hh + 1) * 64)
                cp(WCf[rows, :, 0, hh, :], WCt[rows, :, :, :].rearrange("p g a b -> p g (a b)"),
                   r=["WCt", "WCf"], w=["WCf0%d" % hh])
            tt(WCt[:], crb, pim_c, ALU.mult, r=Ck + ["PIM", "WCf00", "WCf01"], w=["WCt"])
            tt(WCt2[:], cib, pre_c, ALU.mult, r=Ck + ["PRE"], w=["WCt2"])
            tt(WCt[:], WCt[:], WCt2[:], ALU.add, r=["WCt", "WCt2"], w=["WCt"])
            for hh in range(2):
                rows = slice(hh * 64, (hh + 1) * 64)
                ts(WCf[rows, :, 1, hh, :], WCt[rows, :, :, :].rearrange("p g a b -> p g (a b)"), -1.0, None,
                   ALU.mult, r=["WCt", "WCf"], w=["WCf1%d" % hh])
            WCk = ["WCf00", "WCf01", "WCf10", "WCf11"]
            if stop_after == "P8":
                return finish(nc, P, es, out_d, es_phase[0])

            for ri in range(2):
                for hh in range(2):
                    cp(SW[:, :, 4 + 2 * ri + hh, :], WCf[:, :, ri, hh, :], r=WCk, w=["SW%d" % (4 + 2 * ri + hh)],
                       eng="scalar")
            memset(SW[:, :, 0:4, :], 0.0, w=["SW0", "SW1", "SW2", "SW3"])
            if stop_after == "P9":
                return finish(nc, P, es, out_d, es_phase[0])

            for gq in range(8):
                for ri in range(2):
                    tp(ps_w[:, ri, :], GRb[:, gq, ri, :], ident_b[:], r=["GRb%d" % ri], w=["ps_w"])
                if stop_after == "P9b":
                    return finish(nc, P, es, out_d, es_phase[0])
                for ri in range(2):
                    for hh in range(2):
                        cs_ = slice(hh * 64, (hh + 1) * 64)
                        cp(SW[:, gq, 2 * ri + hh, cs_], ps_w[:, ri, cs_], r=["ps_w"],
                           w=["SW%d" % (2 * ri + hh)], eng=CPENG[hh])
                if stop_after == "P10":
                    return finish(nc, P, es, out_d, es_phase[0])
                for hh in range(2):
                    mm(ps_z[:, hh, :], GR[:, gq, 8:16, :].rearrange("p a b -> p (a b)"), WCf[:, gq, 0, hh, :],
                       True, False, r=["GR"] + WCk, w=["ps_z"])
                    mm(ps_z[:, hh, :], GI[:, gq, 8:16, :].rearrange("p a b -> p (a b)"), WCf[:, gq, 1, hh, :],
                       False, True, r=["GI"] + WCk, w=["ps_z"])
                    g = 2 * (k4 * 8 + gq) + hh
                    tt(tzt[:, hh, :], ps_z[:, hh, :], trit[:], ALU.mult, r=["ps_z", "trit"], w=["tzt%d" % hh])
                    stt(SW[:, gq, 8 + hh, :], ident_f[:], DSK[:, g:g + 1], tzt[:, hh, :], ALU.mult, ALU.add,
                        r=["tzt%d" % hh, "DSK"], w=["SW%d" % (8 + hh)])
            if stop_after == "P11":
                return finish(nc, P, es, out_d, es_phase[0])
            memset(SW[:, :, 10:12, :], 0.0, w=["SW10"])
            allk = ["SW%d" % i for i in range(11)]
            dmaP(SSMW_d[:, k4 * 8 * NSL * 128:(k4 + 1) * 8 * NSL * 128], SW[:].rearrange("p a b c -> p (a b c)"),
                r=allk)
        if "dbg_PRE" in dbg_d:
            dbg_dump("dbg_PRE", PRE[:].rearrange("p a b -> p (a b)"))
        if "dbg_PIM" in dbg_d:
            dbg_dump("dbg_PIM", PIM[:].rearrange("p a b -> p (a b)"))
    if stop_after == "P":
        if "dbg_SSMW" in dbg_d:
            dma(dbg_d["dbg_SSMW"], SSMW_d)
        return finish(nc, P, es, out_d, None)


    es_phaseW = ExitStack()
    es_phase[0] = es_phaseW
    if True:
        CW = 2304
        stg = [A("w_stg%d" % i, [128, CW], F32) for i in range(2)]
        stb = [A("w_stb%d" % i, [128, CW], BF16) for i in range(2)]
        jobs = []
        for rt in range(8):
            for c in range(4):
                jobs.append((win_d[rt * 128:(rt + 1) * 128, c * CW:(c + 1) * CW],
                             Wb_in[rt * 128:(rt + 1) * 128, c * CW:(c + 1) * CW], CW))
        for src, dst, nr in ((wglu_d, Wb_glu, 8), (wso_d, Wb_so, 8), (wao_d, Wb_ao, 4), (wo_d, Wb_o, 8)):
            for rt in range(nr):
                jobs.append((src[rt * 128:(rt + 1) * 128, :], dst[rt * 128:(rt + 1) * 128, :], D))
        for i, (src, dst, n) in enumerate(jobs):
            b = i % 2
            dma(stg[b][:, 0:n], src, w=["wstg%d" % b])
            cp(stb[b][:, 0:n], stg[b][:, 0:n], r=["wstg%d" % b], w=["wstb%d" % b],
               eng="gpsimd")
            dma(dst, stb[b][:, 0:n], r=["wstb%d" % b], q="gpsimd")
    P.barrier()
    es_phaseW.close()
    es_phaseP.close()

    hT = AG("hT", [128, DT, L], BF16)
    cos2 = AG("cos2", [128, NT, 16], F32)
    sin2 = AG("sin2", [128, NT, 16], F32)

    for sq in range(NSEQ):
        es_phase[0] = ExitStack()
        cosT = A("r_cosT", [128, NT, 8], F32)
        sinT = A("r_sinT", [128, NT, 8], F32)
        posi = A("r_posi", [NT, 128], I32)
        posf = A("r_posf", [NT, 128], F32)
        dma(posi[:], pos_d[sq], w=["posi"])
        cp(posf[:], posi[:], r=["posi"], w=["posf"])
        ps_p = psum("r_ps_p", [128, NT], F32)
        tp(ps_p[:], posf[:], ident_f[0:NT, 0:NT], r=["posf"], w=["ps_p"])
        posT = A("r_posT", [128, NT], F32)
        cp(posT[:], ps_p[:], r=["ps_p"], w=["posT"])
        rang = A("r_ang", [128, NT, 8], F32)
        tt(rang[:], posT[:].unsqueeze(2).to_broadcast([128, NT, 8]),
           invf[:].unsqueeze(1).to_broadcast([128, NT, 8]), ALU.mult, r=["posT"], w=["rRang"])
        range_reduce_sincos(rang[:].rearrange("p a b -> p (a b)"), sinT[:].rearrange("p a b -> p (a b)"),
                            cosT[:].rearrange("p a b -> p (a b)"), [128, NT * 8], "rR%d" % sq if False else "rR")
        cp(cos2[:, :, 0:8], cosT[:], r=["rRout" + str(PI / 2)], w=["cos2a"])
        cp(cos2[:, :, 8:16], cosT[:], r=["rRout" + str(PI / 2)], w=["cos2b"])
        ts(sin2[:, :, 0:8], sinT[:], -1.0, None, ALU.mult, r=["rRout0.0"], w=["sin2a"])
        cp(sin2[:, :, 8:16], sinT[:], r=["rRout0.0"], w=["sin2b"])
        P.barrier()
        es_phase[0].close()
        es_phase[0] = ExitStack()
        normw_rep = A("n_normw_rep", [128, D], F32)
        dma(normw_rep[:], normw_d.partition_broadcast(128), w=["normw"])
        xt = [A("n_x%d" % i, [128, D], F32) for i in range(3)]
        xsq = A("n_xsq", [128, D], F32)
        hb = [A("n_hb%d" % i, [128, D], BF16) for i in range(3)]
        ssq = [A("n_ss%d" % i, [128, 1], F32) for i in range(3)]
        rst = [A("n_rs%d" % i, [128, 1], F32) for i in range(3)]
        ps_h = [psum("n_ps_h%d" % i, [128, DT, 128], BF16) for i in range(2)]

        def n_s1(t):
            b = t % 3
            dma(xt[b][:], x_d[sq, t * 128:(t + 1) * 128, :], w=["xt%d" % b])
            act(xsq[:], xt[b][:], AF.Square, r=["xt%d" % b], w=["xsq", "ss%d" % b], accum_out=ssq[b][:])
            act(rst[b][:], ssq[b][:], AF.Sqrt, r=["ss%d" % b], w=["rs%d" % b], bias=EPS_AP[:], scale=1.0 / D)
            recip(rst[b][:], rst[b][:], r=["rs%d" % b], w=["rs%d" % b])
            stt(hb[b][:], xt[b][:], rst[b][:, 0:1], normw_rep[:], ALU.mult, ALU.mult,
                r=["xt%d" % b, "rs%d" % b, "normw"], w=["hb%d" % b])

        def n_s2(t):
            b = t % 3
            pb = t % 2
            for dt_ in range(DT):
                tp(ps_h[pb][:, dt_, :], hb[b][:, dt_ * 128:(dt_ + 1) * 128], ident_b[:], r=["hb%d" % b],
                   w=["ps_h%d" % pb])
            cp(hT[:, :, t * 128:(t + 1) * 128], ps_h[pb][:], r=["ps_h%d" % pb], w=["hT"],
               eng="scalar" if t % 2 else "vector")

        for i in range(NT + 1):
            if i < NT:
                n_s1(i)
            if i >= 1:
                n_s2(i - 1)
        P.barrier()
        es_phase[0].close()
        if stop_after == "N":
            for nm, src_ in (("dbg_hT", hT[:].rearrange("p a b -> p (a b)")),
                             ("dbg_cos", cosT[:].rearrange("p a b -> p (a b)")),
                             ("dbg_sin", sinT[:].rearrange("p a b -> p (a b)"))):
                if nm in dbg_d:
                    dma(dbg_d[nm], src_)
            return finish(nc, P, es, out_d, None)

        es_phase[0] = ExitStack()
        DILS = (1, 4, 16)
        NB = L // 128
        Wqk = A("a_Wqk", [128, DT, 384], BF16)
        Wv = A("a_Wv", [128, DT, 192], BF16)
        sq_s = [A("a_sq%d" % i, [128, 6, 64], F32) for i in range(3)]
        ss6 = [A("a_ss6%d" % i, [128, 6], F32) for i in range(3)]
        rstd6 = [A("a_rstd6%d" % i, [128, 6], F32) for i in range(3)]
        qn = [A("a_qn%d" % i, [128, 6, 64], F32) for i in range(3)]
        qb = [A("a_qb%d" % i, [128, 6, 64], BF16) for i in range(3)]
        rr4 = [[A("a_r%d_%d" % (j, i), [128, 6, 16], F32) for j in range(2)] for i in range(3)]
        QK01 = A("a_QK01", [128, 2, L], BF16)
        QK2 = A("a_QK2", [64, 2, L], BF16)
        Vg = [A("a_V%d" % g, [128, NB, 65], BF16) for g in range(3)]
        Oacc = A("a_Oacc", [65, L], F32)
        Pm = [A("a_Pm%d" % i, [128, 512], BF16) for i in range(4)]
        yA = A("a_yA", [64, L], BF16)
        ps_T = [psum("a_ps_T%d" % i, [128, 512], BF16) for i in range(2)]
        FB = [psum("a_F%d" % i, [128, 512], F32) for i in range(6)]
        FK = ["psF%d" % i for i in range(6)]
        QB3 = [0, 1, 4]
        ps_s = [FB[0], FB[1], FB[4]]
        ps_sk = [FK[0], FK[1], FK[4]]
        DUMMY = int(os.environ.get("DUMMY", "0"))

        def pe_warm(nrep):
            for _ in range(nrep):
                mm(FB[5][:, 0:512], ident_b[:], amask[:], True, True, r=[], w=[FK[5]])
        ps_o = [FB[2], FB[3]]
        ps_ok = [FK[2], FK[3]]
        ps_b = FB[5]
        for g in range(3):
            memset(Vg[g][:, :, 64:65], 1.0, w=["V%d" % g])
        pcount = [0]
        ocount = [0]
        vcount = [0]
        NHL = 0 if SKIPA else (NHEADS if stop_after != "A1" else 1)

        rbn = [A("a_rbn%d" % i, [64, 512], F32) for i in range(2)]

        def norm_block(hh, cb):
            cs_ = slice(cb * 512, (cb + 1) * 512)
            i = cb % 2
            mm(ps_b[0:64, :], ones_f[64:65, 0:64], Oacc[64:65, cs_], True, True, r=["Oacc"], w=[FK[5]])
            recip(rbn[i][:], ps_b[0:64, :], r=[FK[5]], w=["rbn%d" % i])
            tt(yA[:, cs_], Oacc[0:64, cs_], rbn[i][:], ALU.mult, r=["rbn%d" % i, "Oacc"], w=["yA"], eng="gpsimd")
            if cb == L // 512 - 1:
                dma(A_d[sq, hh * 64:(hh + 1) * 64, :], yA[:], r=["yA"], q="gpsimd")

        for h in range(NHL):
            for sl in range(6):
                c0 = (2048 if sl < 3 else 3584) + (sl % 3) * 512 + h * 64
                dma(Wqk[:, :, sl * 64:(sl + 1) * 64],
                    Wb_in[:, c0:c0 + 64].rearrange("(dt p) c -> p dt c", p=128), w=["Wqk"])
            for g in range(3):
                c0 = 5120 + g * 512 + h * 64
                dma(Wv[:, :, g * 64:(g + 1) * 64],
                    Wb_in[:, c0:c0 + 64].rearrange("(dt p) c -> p dt c", p=128), w=["Wv"])
            vjobs = []
            for g in range(3):
                for kb0 in range(0, NB, 8):
                    vjobs.append((g, kb0))

            def do_vjob(g, kb0):
                dil = DILS[g]
                nb = L // (128 * dil)
                vb = vcount[0] % 2
                vcount[0] += 1
                pv_ = FB[2 + vb][:].rearrange("p (a b) -> p a b", a=8)
                for j in range(8):
                    kb = kb0 + j
                    r_, i_ = kb // nb, kb % nb
                    o_ = r_ + dil * 128 * i_
                    for dt_ in range(DT):
                        mm(pv_[:, j, :], hT[:, dt_, o_:o_ + dil * 127 + 1:dil], Wv[:, dt_, g * 64:(g + 1) * 64],
                           dt_ == 0, dt_ == DT - 1, r=["Wv"], w=[FK[2 + vb]])
                cp(Vg[g][:, kb0:kb0 + 8, 0:64], pv_, r=[FK[2 + vb]], w=["V%d" % g], eng="vector")

            vdone = 0

            def qk_mm(t):
                fb = QB3[t % 3]
                for dt_ in range(DT):
                    mm(FB[fb][:, 0:384], hT[:, dt_, t * 128:(t + 1) * 128],
                       Wqk[:, dt_, :], dt_ == 0, dt_ == DT - 1, r=["Wqk"], w=[FK[fb]])

            def qk_s1(t):
                b = t % 3
                pq3 = FB[QB3[b]][:, 0:384].rearrange("p (a b) -> p a b", a=6)
                kq = FK[QB3[b]]
                B_ = str(b)
                act(sq_s[b][:], pq3, AF.Square, r=[kq], w=["sq_s" + B_])
                P.op("vector", lambda e, b=b: e.tensor_reduce(ss6[b][:], sq_s[b][:], AX.X, ALU.add), r=["sq_s" + B_],
                     w=["ss6" + B_])
                act(rstd6[b][:], ss6[b][:], AF.Sqrt, r=["ss6" + B_], w=["rstd6" + B_], bias=EPS_AP[:], scale=1.0 / 64)
                recip(rstd6[b][:], rstd6[b][:], r=["rstd6" + B_], w=["rstd6" + B_])
                tt(qn[b][:], pq3, rstd6[b][:].unsqueeze(2).to_broadcast([128, 6, 64]), ALU.mult,
                   r=[kq, "rstd6" + B_], w=["qn" + B_])
                tt(qn[b][:], qn[b][:], qkw_rep[:], ALU.mult, r=["qn" + B_], w=["qn" + B_], eng="gpsimd")

            def qk_s2(t):
                b = t % 3
                B_ = str(b)
                c2_ = cos2[:, t, :].unsqueeze(1).to_broadcast([128, 6, 16])
                s2_ = sin2[:, t, :].unsqueeze(1).to_broadcast([128, 6, 16])
                xs = qn[b][:, :, 0:16]
                xr = sap(qn[b], 0, 128, 8, [[64, 6], [-8, 2], [1, 8]])
                r0, r1 = rr4[b][0], rr4[b][1]
                tt(r0[:], xs, c2_, ALU.mult, r=["qn" + B_], w=["r0" + B_])
                tt(r1[:].rearrange("p a (h e) -> p a h e", h=2), xr,
                   sin2[:, t, :].rearrange("p (h e) -> p h e", h=2).unsqueeze(1).to_broadcast([128, 6, 2, 8]),
                   ALU.mult, r=["qn" + B_], w=["r1" + B_])
                cp(qb[b][:, :, 16:64], qn[b][:, :, 16:64], r=["qn" + B_], w=["qb2" + B_], eng="scalar")
                tt(qb[b][:, :, 0:16], r0[:], r1[:], ALU.add, r=["r0" + B_, "r1" + B_], w=["qb0" + B_])

            def qk_tp(t):
                b = t % 3
                tb = t % 2
                B_ = str(b)
                qbf = qb[b][:].rearrange("p a b -> p (a b)")
                QB = ["qb0" + B_, "qb2" + B_]
                kT = "psT" + str(tb)
                tp(ps_T[tb][:, 0:128], qbf[:, 0:128], ident_b[:], r=QB, w=[kT])
                tp(ps_T[tb][:, 128:256], qbf[:, 192:320], ident_b[:], r=QB, w=[kT])
                tp(ps_T[tb][0:64, 256:384], qbf[:, 128:192], ident_b[:], r=QB, w=[kT])
                tp(ps_T[tb][0:64, 384:512], qbf[:, 320:384], ident_b[:], r=QB, w=[kT])
                cp(QK01[:, :, t * 128:(t + 1) * 128], ps_T[tb][:, 0:256].rearrange("p (a b) -> p a b", a=2),
                   r=[kT], w=["QK"], eng="scalar")
                cp(QK2[:, :, t * 128:(t + 1) * 128], ps_T[tb][0:64, 256:512].rearrange("p (a b) -> p a b", a=2),
                   r=[kT], w=["QK"], eng="scalar")

            qk_mm(0)
            qk_mm(1)
            for i in range(NT + 2):
                if i + 2 < NT:
                    qk_mm(i + 2)
                if h > 0 and 2 <= i < 2 + L // 512:
                    norm_block(h - 1, i - 2)
                if i < NT:
                    qk_s1(i)
                if 0 <= i - 1 < NT:
                    qk_s2(i - 1)
                if DUMMY and not (h > 0 and 1 <= i < 3 + L // 512):
                    pe_warm(DUMMY)
                want = (min(i + 1, NT) * len(vjobs)) // NT
                while vdone < want:
                    do_vjob(*vjobs[vdone])
                    vdone += 1
                if 0 <= i - 2 < NT:
                    qk_tp(i - 2)
            pairs = []
            for g in range(3):
                dil = DILS[g]
                nb = L // (128 * dil)
                for r_ in range(dil):
                    for pp in range((nb + 1) // 2):
                        pairs.append((g, r_, pp))

            def qkviews(g):
                if g < 2:
                    return QK01[g * 64:(g + 1) * 64, 0, :], QK01[g * 64:(g + 1) * 64, 1, :]
                return QK2[0:64, 0, :], QK2[0:64, 1, :]

            def pair_geom(g, r_, pp):
                dil = DILS[g]
                nb = L // (128 * dil)
                i0, i1 = 2 * pp, 2 * pp + 1
                a = 256 if i0 < nb - 1 else 128
                b = 0 if i1 >= nb else (256 if i1 < nb - 1 else 128)
                return dil, nb, i0, i1, a, b

            def emit_S(n):
                g, r_, pp = pairs[n]
                dil, nb, i0, i1, a, b = pair_geom(g, r_, pp)
                Qv, Kv = qkviews(g)
                sb = n % 3
                o0 = r_ + dil * 128 * i0
                mm(ps_s[sb][:, 0:a], Kv[:, o0:o0 + dil * 127 + 1:dil], Qv[:, o0:o0 + dil * (a - 1) + 1:dil], True, True,
                   r=["QK"], w=[ps_sk[sb]])
                if b:
                    o1 = r_ + dil * 128 * i1
                    mm(ps_s[sb][:, 256:256 + b], Kv[:, o1:o1 + dil * 127 + 1:dil],
                       Qv[:, o1:o1 + dil * (b - 1) + 1:dil], True, True, r=["QK"], w=[ps_sk[sb]])

            def emit_E(n):
                g, r_, pp = pairs[n]
                dil, nb, i0, i1, a, b = pair_geom(g, r_, pp)
                sb = n % 3
                pm = n % 4
                nn = 256 + b if b else a
                act(Pm[pm][:, 0:nn], ps_s[sb][:, 0:nn], AF.Exp, r=[ps_sk[sb]], w=["Pm%d" % pm], scale=0.125)
                tt(Pm[pm][:, 0:nn], Pm[pm][:, 0:nn], amask[:, 0:nn], ALU.mult, r=["Pm%d" % pm], w=["Pm%d" % pm],
                   eng="gpsimd" if n % 3 else "vector")

            ob = None
            NP_ = len(pairs)
            emit_S(0)
            if NP_ > 1:
                emit_S(1)
            emit_E(0)
            for n in range(NP_):
                g, r_, pp = pairs[n]
                dil, nb, i0, i1, a, b = pair_geom(g, r_, pp)
                if n + 2 < NP_:
                    emit_S(n + 2)
                if n + 1 < NP_:
                    emit_E(n + 1)
                pm = n % 4
                prevPm = (n - 1) % 4
                for (i_, curc, prevsrc) in ((i0, 0, (prevPm, 384)), (i1, 256, (pm, 128))):
                    if i_ >= nb:
                        continue
                    slot = i_ % 4
                    if slot == 0:
                        ob = ocount[0] % 2
                        ocount[0] += 1
                    kbi = r_ * nb + i_
                    outp = ps_o[ob][0:65, slot * 128:(slot + 1) * 128]
                    if i_ > 0:
                        ppm, pc = prevsrc
                        mm(outp, Vg[g][:, kbi - 1, :], Pm[ppm][:, pc:pc + 128], True, False,
                           r=["V%d" % g, "Pm%d" % ppm], w=[ps_ok[ob]])
                        mm(outp, Vg[g][:, kbi, :], Pm[pm][:, curc:curc + 128], False, True,
                           r=["V%d" % g, "Pm%d" % pm], w=[ps_ok[ob]])
                    else:
                        mm(outp, Vg[g][:, kbi, :], Pm[pm][:, curc:curc + 128], True, True,
                           r=["V%d" % g, "Pm%d" % pm], w=[ps_ok[ob]])
                    if slot == 3 or i_ == nb - 1:
                        nq = slot + 1
                        ib = i_ - slot
                        ov = sap(Oacc, 0, 65, r_ + dil * 128 * ib, [[dil * 128, nq], [dil, 128]])
                        pv = ps_o[ob][0:65, 0:nq * 128].rearrange("p (a b) -> p a b", a=nq)
                        if g == 0:
                            cp(ov, pv, r=[ps_ok[ob]], w=["Oacc"], eng="scalar")
                        else:
                            tt(ov, pv, ov, ALU.add, r=[ps_ok[ob]], w=["Oacc"])
            if h == NHL - 1:
                for cb in range(L // 512):
                    norm_block(h, cb)
        P.barrier()
        es_phase[0].close()
        if stop_after in ("A", "A1"):
            if "dbg_A" in dbg_d:
                dma(dbg_d["dbg_A"], A_d[sq])
            return finish(nc, P, es, out_d, None)

        es_phase[0] = ExitStack()
        RB = [A("s_R%d" % i, [128, 8192], BF16) for i in range(3)]
        RK = ["R0", "R1", "R2"]
        X8 = A("s_X8", [128, 129, 2, 32], F32)
        sc1 = A("s_sc1", [128, 2, 32], F32)
        sc2 = A("s_sc2", [128, 2, 32], F32)
        Hb = [A("s_Hb%d" % i, [128, 2, 128], BF16) for i in range(2)]
        SWb = [A("s_SWb%d" % i, [128, 4, 128], BF16) for i in range(4)]
        SWc = [A("s_SWc%d" % i, [128, 6, 128], BF16) for i in range(4)]
        WS = [A("s_WS%d" % i, [128, 8, 256], BF16) for i in range(2)]
        gsq_ = [A("s_gsq%d" % i, [128, 512], F32) for i in range(2)]
        gt2_ = [A("s_gt2%d" % i, [128, 512], F32) for i in range(2)]
        gsg_ = [A("s_gsg%d" % i, [128, 512], F32) for i in range(2)]
        sgb = [A("s_sgb%d" % i, [128, 512], BF16) for i in range(2)]
        gate4 = [A("s_gate%d" % i, [128, 512], BF16) for i in range(4)]
        srf_ = gsq_
        Abt = A("s_Abt", [128, 4, 1024], BF16)
        xt_ = [A("s_xt%d" % i, [128, 256], F32) for i in range(4)]
        ot_ = [A("s_ot%d" % i, [128, 256], F32) for i in range(4)]
        ps_a = [psum("s_ps_a%d" % i, [128, 512], F32) for i in range(2)]
        ps_t = [psum("s_ps_t%d" % i, [128, 1024], BF16) for i in range(4)]
        ps_y = [psum("s_ps_y%d" % i, [128, 4, 128], F32) for i in range(2)]
        wsc = [0]
        cnt = [0]
        tbc = [0]
        tcnt = [0]
        TB = [ps_a[0], ps_a[1],
              ps_y[0][:].rearrange("p a b -> p (a b)"), ps_y[1][:].rearrange("p a b -> p (a b)")]
        TK = ["ps_a0", "ps_a1", "ps_y0", "ps_y1"]
        NTB = 4

        def v_g(i):
            return RB[i][:].rearrange("p (g c) -> p g c", g=64)

        def v_t(i):
            return RB[i][:].rearrange("p (a t) -> p a t", a=8)

        def bmap(b):
            return ((1 + b) % 3, (2 + b) % 3, (0 + b) % 3)

        def load_ws(src, ktiles, c0):
            b = wsc[0] % 2
            wsc[0] += 1
            dma(WS[b][:, 0:ktiles, :], src[:, c0:c0 + 256].rearrange("(kt p) c -> p kt c", p=128), w=["WS%d" % b])
            return b

        def alt():
            cnt[0] += 1
            return "scalar" if cnt[0] % 2 else "vector"

        def tmpb():
            tcnt[0] += 1
            return tcnt[0] % 2

        def fm_proj(ktiles, rhs_fn, rkeys, c2, wb):
            k = tbc[0] % NTB
            tbc[0] += 1
            for kt in range(ktiles):
                mm(TB[k][:, 0:512], WS[wb][:, kt, c2 * 128:(c2 + 1) * 128], rhs_fn(kt), kt == 0, kt == ktiles - 1,
                   r=["WS%d" % wb] + rkeys, w=[TK[k]])
            return k

        def front(blk):
            X, Y, Z = bmap(blk)
            T0 = blk * 1024
            Ib, Ik = RB[Z], RK[Z]
            Uv, Uk = v_g(X), RK[X]
            if blk > 0:
                cp(X8[:, 0, :, :], X8[:, 128, :, :], r=["X8"], w=["X8"])
            for cq in range(4):
                wb = load_ws(Wb_in, 8, cq * 256)
                for s in range(8):
                    k = (cq * 8 + s) % 2
                    for dt_ in range(DT):
                        mm(ps_a[k][:, 0:256], hT[:, dt_, T0 + s:T0 + s + 8 * 127 + 1:8], WS[wb][:, dt_, :],
                           dt_ == 0, dt_ == DT - 1, r=["WS%d" % wb], w=["ps_a%d" % k])
                    io = sap(Ib, 0, 128, cq * 16 * 128 + s * 16, [[128, 16], [1, 16]])
                    cp(io, ps_a[k][:, 0:256].rearrange("p (a b) -> p a b", a=16), r=["ps_a%d" % k], w=[Ik], eng=alt())
            for g8 in range(8):
                k = g8 % 4
                for j in range(8):
                    g = g8 * 8 + j
                    tp(ps_t[k][:, j * 128:(j + 1) * 128], Ib[:, g * 128:(g + 1) * 128], ident_b[:], r=[Ik],
                       w=["ps_t%d" % k])
                cp(Uv[:, g8 * 8:(g8 + 1) * 8, :], ps_t[k][:].rearrange("p (a b) -> p a b", a=8), r=["ps_t%d" % k],
                   w=[Uk], eng=alt())
            for gp2 in range(16):
                pxb = ps_y[gp2 % 2][:].rearrange("p (a b) c -> p a b c", a=2)
                pxk = "ps_y%d" % (gp2 % 2)
                for j in range(2):
                    gp = gp2 * 2 + j
                    sb = gp % 4
                    dma(SWb[sb][:], SSMW_d[:, gp * NSL * 128:gp * NSL * 128 + 4 * 128].rearrange("p (a b) -> p a b", a=4),
                        w=["SWb%d" % sb])
                    for ri in range(2):
                        mm(pxb[:, j, ri, :], SWb[sb][:, 2 * ri, :], Uv[:, 2 * gp, :], True, False,
                           r=["SWb%d" % sb, Uk], w=[pxk])
                        mm(pxb[:, j, ri, :], SWb[sb][:, 2 * ri + 1, :], Uv[:, 2 * gp + 1, :], False, True,
                           r=["SWb%d" % sb, Uk], w=[pxk])
                xo = sap(X8, 0, 128, 64 + gp2 * 2, [[1, 2], [32, 2], [64, 128]])
                cp(xo, pxb[:], r=[pxk], w=["X8"], eng=alt())

        def scan_step(c):
            cur = X8[:, c, :, :]
            swp = sap(X8, 0, 128, c * 64 + 32, [[-32, 2], [1, 32]])
            nxt = X8[:, c + 1, :, :]
            tt(sc1[:], cur, a8[:, 0, :, :], ALU.mult, r=["X8"], w=["sc1"])
            tt(sc2[:], swp, a8[:, 1, :, :], ALU.mult, r=["X8"], w=["sc2"])
            tt(sc1[:], sc1[:], sc2[:], ALU.add, r=["sc1", "sc2"], w=["sc1"])
            tt(nxt, nxt, sc1[:], ALU.add, r=["sc1", "X8"], w=["X8"])

        def cstep(blk):
            X, Y, Z = bmap(blk)
            Uv, Uk = v_g(X), RK[X]
            YTb, YTk = RB[Y], RK[Y]
            YTv = v_g(Y)
            YIb, YIk = RB[Z], RK[Z]

            CB = [ps_y[0][:].rearrange("p a b -> p (a b)"), ps_y[1][:].rearrange("p a b -> p (a b)"),
                  ps_a[0][:, 0:512], ps_a[1][:, 0:512]]
            CK = ["ps_y0", "ps_y1", "ps_a0", "ps_a1"]

            def cstep_mm(gp2):
                k = gp2 % 4
                pyv = CB[k].rearrange("p (a b) -> p a b", a=4)
                for j in range(2):
                    gp = gp2 * 2 + j
                    sb = gp % 4
                    dma(SWc[sb][:], SSMW_d[:, gp * NSL * 128 + 4 * 128:gp * NSL * 128 + 10 * 128].rearrange(
                        "p (a b) -> p a b", a=6), w=["SWc%d" % sb])
                    hin = sap(X8, 0, 128, gp, [[32, 2], [64, 128]])
                    hb = gp % 2
                    cp(Hb[hb][:], hin, r=["X8"], w=["Hb%d" % hb], eng="gpsimd")
                    for g2 in range(2):
                        g = 2 * gp + g2
                        o_ = pyv[:, 2 * j + g2, :]
                        mm(o_, SWc[sb][:, 4 + g2, :], Uv[:, g, :], True, False, r=["SWc%d" % sb, Uk], w=[CK[k]])
                        mm(o_, SWc[sb][:, 0 + g2, :], Hb[hb][:, 0, :], False, False, r=["SWc%d" % sb, "Hb%d" % hb],
                           w=[CK[k]])
                        mm(o_, SWc[sb][:, 2 + g2, :], Hb[hb][:, 1, :], False, True, r=["SWc%d" % sb, "Hb%d" % hb],
                           w=[CK[k]])

            def gelu_a(gp2):
                k = gp2 % 2
                py = CB[gp2 % 4]
                pk = CK[gp2 % 4]
                K_ = str(k)
                act(gsq_[k][:], py, AF.Square, r=[pk], w=["gsq" + K_], scale=math.sqrt(0.044715))
                stt(gt2_[k][:], gsq_[k][:], 1.0, py, ALU.add, ALU.mult, r=["gsq" + K_, pk], w=["gt2" + K_])

            def gelu_b(gp2):
                k = gp2 % 2
                py = CB[gp2 % 4]
                pk = CK[gp2 % 4]
                K_ = str(k)
                act(gsg_[k][:], gt2_[k][:], AF.Sigmoid, r=["gt2" + K_], w=["gsg" + K_], scale=1.5957691216057308)
                tt(YTb[:, gp2 * 512:(gp2 + 1) * 512], gsg_[k][:], py, ALU.mult, r=["gsg" + K_, pk], w=[YTk])

            for g0 in range(3):
                cstep_mm(g0)
            for gp2 in range(16):
                if gp2 > 0:
                    gelu_b(gp2 - 1)
                if gp2 + 3 < 16:
                    cstep_mm(gp2 + 3)
                gelu_a(gp2)
            gelu_b(15)
            for g8 in range(8):
                k = g8 % 4
                for j in range(8):
                    g = g8 * 8 + j
                    tp(ps_t[k][:, j * 128:(j + 1) * 128], YTv[:, g, :], ident_b[:], r=[YTk], w=["ps_t%d" % k])
                yo = sap(YIb, 0, 128, 16 * g8 * 8, [[16, 8], [1024, 8], [1, 16]])
                cp(yo, ps_t[k][:].rearrange("p (a b c) -> p a b c", a=8, b=8), r=["ps_t%d" % k], w=[YIk], eng=alt())
            YIv = v_t(Z)
            for ct in range(8):
                k = ct % 4
                for t in range(8):
                    tp(ps_t[k][:, t * 128:(t + 1) * 128], YIv[:, t, ct * 128:(ct + 1) * 128], ident_b[:], r=[YIk],
                       w=["ps_t%d" % k])
                go = sap(RB[X], 0, 128, ct * 1024, [[1, 8], [8, 128]])
                cp(go, ps_t[k][:].rearrange("p (a b) -> p a b", a=8), r=["ps_t%d" % k], w=[RK[X]], eng=alt())

        def tail(blk, mid_hook, fill_hook):
            X, Y, Z = bmap(blk)
            gTv, gK = v_t(X), RK[X]
            S1v, sK = v_t(Y), RK[Y]
            mv, mK = v_t(Z), RK[Z]
            T0 = blk * 1024
            gblk = slice(T0, T0 + 1024)
            dma(Abt[:], A_d[sq, :, gblk].rearrange("(kt p) t -> p kt t", p=128), w=["Abt"])
            halves = [(slice(0, 512), slice(T0, T0 + 512)), (slice(512, 1024), slice(T0 + 512, T0 + 1024))]
            for cq in range(4):
                wb = load_ws(Wb_glu, 8, cq * 256)
                for c2 in range(2):
                    co = 2 * cq + c2
                    for (tk, gtk) in halves:
                        k = fm_proj(8, lambda kt: gTv[:, kt, tk], [gK], c2, wb)
                        i = tmpb()
                        act(sgb[i][:], TB[k][:, 0:512], AF.Sigmoid, r=[TK[k]], w=["sgb%d" % i])
                        tt(S1v[:, co, tk], gTv[:, co, tk], sgb[i][:], ALU.mult, r=[gK, "sgb%d" % i], w=[sK])
            for cq in range(4):
                wb = load_ws(Wb_in, 8, 1024 + cq * 256)
                for c2 in range(2):
                    co = 2 * cq + c2
                    for (tk, gtk) in halves:
                        k = fm_proj(8, lambda kt: hT[:, kt, gtk], [], c2, wb)
                        i = tmpb()
                        act(srf_[i][:], TB[k][:, 0:512], AF.Sigmoid, r=[TK[k]], w=["gsq%d" % i])
                        tt(gt2_[i][:], TB[k][:, 0:512], srf_[i][:], ALU.mult, r=[TK[k], "gsq%d" % i], w=["gt2%d" % i])
                        tt(S1v[:, co, tk], S1v[:, co, tk], gt2_[i][:], ALU.mult, r=["gt2%d" % i], w=[sK], eng="gpsimd")
            for cq in range(4):
                wb2 = load_ws(Wb_in, 8, 7168 + cq * 256)
                gi = 0
                for c2 in range(2):
                    for (tk, gtk) in halves:
                        k2 = fm_proj(8, lambda kt: hT[:, kt, gtk], [], c2, wb2)
                        act(gate4[gi][:], TB[k2][:, 0:512], AF.Sigmoid, r=[TK[k2]], w=["gate%d" % gi])
                        gi += 1
                wb = load_ws(Wb_so, 8, cq * 256)
                gi = 0
                for c2 in range(2):
                    co = 2 * cq + c2
                    for (tk, gtk) in halves:
                        k = fm_proj(8, lambda kt: S1v[:, kt, tk], [sK], c2, wb)
                        tt(mv[:, co, tk], TB[k][:, 0:512], gate4[gi][:], ALU.mult, r=[TK[k], "gate%d" % gi], w=[mK])
                        gi += 1
            mid_hook()
            for cq in range(2):
                wb = load_ws(Wb_in, 8, 6656 + cq * 256)
                for c2 in range(2):
                    kt_ = 2 * cq + c2
                    for (tk, gtk) in halves:
                        k = fm_proj(8, lambda kt: hT[:, kt, gtk], [], c2, wb)
                        i = tmpb()
                        act(gt2_[i][:], TB[k][:, 0:512], AF.Silu, r=[TK[k]], w=["gt2%d" % i])
                        tt(Abt[:, kt_, tk], Abt[:, kt_, tk], gt2_[i][:], ALU.mult, r=["gt2%d" % i, "Abt"], w=["Abt"],
                           eng="gpsimd")
                        fill_hook()
            for cq in range(4):
                wb2 = load_ws(Wb_in, 8, 8192 + cq * 256)
                gi = 0
                for c2 in range(2):
                    for (tk, gtk) in halves:
                        k2 = fm_proj(8, lambda kt: hT[:, kt, gtk], [], c2, wb2)
                        act(gate4[gi][:], TB[k2][:, 0:512], AF.Sigmoid, r=[TK[k2]], w=["gate%d" % gi])
                        gi += 1
                        fill_hook()
                wb = load_ws(Wb_ao, 4, cq * 256)
                gi = 0
                for c2 in range(2):
                    co = 2 * cq + c2
                    for (tk, gtk) in halves:
                        k = fm_proj(4, lambda kt: Abt[:, kt, tk], ["Abt"], c2, wb)
                        i = tmpb()
                        cp(gt2_[i][:], TB[k][:, 0:512], r=[TK[k]], w=["gt2%d" % i], eng="scalar")
                        tt(gt2_[i][:], gt2_[i][:], gate4[gi][:], ALU.mult, r=["gt2%d" % i, "gate%d" % gi],
                           w=["gt2%d" % i], eng="gpsimd")
                        tt(mv[:, co, tk], gt2_[i][:], mv[:, co, tk], ALU.add, r=["gt2%d" % i], w=[mK], eng="gpsimd")
                        gi += 1
                        fill_hook()
            for cq in range(4):
                wb = load_ws(Wb_o, 8, cq * 256)
                for t8 in range(8):
                    k = tbc[0] % NTB
                    tbc[0] += 1
                    xb = t8 % 4
                    tok0 = T0 + t8 * 128
                    dma(xt_[xb][:], x_d[sq, tok0:tok0 + 128, cq * 256:(cq + 1) * 256], w=["xt%d" % xb])
                    for kt in range(8):
                        mm(TB[k][:, 0:256], mv[:, kt, t8 * 128:(t8 + 1) * 128],
                           WS[wb][:, kt, :], kt == 0, kt == 7, r=["WS%d" % wb, mK], w=[TK[k]])
                    tt(ot_[xb][:], TB[k][:, 0:256], xt_[xb][:], ALU.add, r=[TK[k], "xt%d" % xb],
                       w=["ot%d" % xb])
                    dma(out_d[sq, tok0:tok0 + 128, cq * 256:(cq + 1) * 256], ot_[xb][:], r=["ot%d" % xb], q="gpsimd")
                    fill_hook()

        memset(X8[:, 0, :, :], 0.0, w=["X8"])
        front(0)
        for c in range(128):
            scan_step(c)
        for blk in range(NBLK):
            cstep(blk)
            pending = []

            def mid_hook(blk=blk, pending=pending):
                if blk + 1 < NBLK:
                    front(blk + 1)
                    pending.extend(range(128))

            def fill_hook(pending=pending):
                for _ in range(2):
                    if pending:
                        scan_step(pending.pop(0))

            tail(blk, mid_hook, fill_hook)
            while pending:
                scan_step(pending.pop(0))
        P.barrier()
        es_phase[0].close()
    return finish(nc, P, es, out_d, None)


def finish(nc, P, es, out_d, ph):
    P.barrier()
    P.emit()
    if ph is not None:
        ph.close()
    es.close()
    return nc


def host_consts():
    c = {}
    c["ident_bf"] = np.eye(128, dtype=np.float32).astype(ml_dtypes.bfloat16)
    c["ident_f"] = np.eye(128, dtype=np.float32)
    j = np.arange(128)[:, None]
    i = np.arange(128)[None, :]
    cur = (j <= i).astype(np.float32)
    prev = (j >= i).astype(np.float32)
    c["amask"] = np.concatenate([cur, prev, cur, prev], axis=1).astype(ml_dtypes.bfloat16)
    s_ = (np.arange(128) // 16)[:, None]
    t_ = (np.arange(128) // 16)[None, :]
    c["trit"] = (t_ >= s_).astype(np.float32)
    c["evals"] = np.tile(np.asarray(EVALS, np.float32)[None, :], (128, 1))
    invf = (500000.0 ** (-np.arange(0, 16, 2, dtype=np.float32) / 16)).astype(np.float32)
    c["invf"] = np.tile(invf[None, :], (128, 1))
    return c


def core_inputs(inputs, seqs, L):
    m = {}
    m["x"] = np.ascontiguousarray(inputs["x"][seqs, :L, :])
    m["pos"] = np.ascontiguousarray(inputs["positions"][seqs, :L]).reshape(len(seqs), L // 128, 128).astype(np.int32)
    m["norm_w"] = np.ascontiguousarray(inputs["norm_w"]).reshape(1, D)
    m["w_in"] = np.ascontiguousarray(inputs["w_in"]).reshape(D, 9216)
    m["lam_re"] = np.ascontiguousarray(inputs["lam_re"]).reshape(32, 128)
    m["lam_im"] = np.ascontiguousarray(inputs["lam_im"]).reshape(32, 128)
    m["log_dt"] = np.ascontiguousarray(inputs["log_dt"]).reshape(32, 2)
    m["b_re"] = np.ascontiguousarray(inputs["b_re"]).reshape(-1)
    m["b_im"] = np.ascontiguousarray(inputs["b_im"]).reshape(-1)
    m["c_re"] = np.ascontiguousarray(inputs["c_re"]).reshape(64, 16, 64)
    m["c_im"] = np.ascontiguousarray(inputs["c_im"]).reshape(64, 16, 64)
    m["d_skip"] = np.ascontiguousarray(inputs["d_skip"]).reshape(64, 16)
    m["w_glu"] = np.ascontiguousarray(inputs["w_glu"]).reshape(D, D)
    m["q_norm_w"] = np.ascontiguousarray(inputs["q_norm_w"]).reshape(1, 192)
    m["k_norm_w"] = np.ascontiguousarray(inputs["k_norm_w"]).reshape(1, 192)
    m["w_ssm_out"] = np.ascontiguousarray(inputs["w_ssm_out"]).reshape(D, D)
    m["w_attn_out"] = np.ascontiguousarray(inputs["w_attn_out"]).reshape(512, D)
    m["w_o"] = np.ascontiguousarray(inputs["w_o"]).reshape(D, D)
    m.update(host_consts())
    return {k: np.asarray(v) for k, v in m.items()}


_NC_CACHE = {}


def kernel(**inputs):
    B, L, _ = inputs["x"].shape
    nseq = B // NCORES
    key = (L, nseq)
    if key not in _NC_CACHE:
        _NC_CACHE[key] = build(L, nseq)
    nc = _NC_CACHE[key]
    in_maps = [core_inputs(inputs, list(range(c * nseq, (c + 1) * nseq)), L) for c in range(NCORES)]
    res = run_bass_kernel_spmd(nc, in_maps, core_ids=list(range(NCORES)))
    out = np.concatenate([np.asarray(r["out"]) for r in res.results], axis=0)
    return out.astype(np.float32)
```
